# Optimizing a Trainium2 kernel written in Bass

```python
import math
import jax, jax.numpy as jnp
from jax import lax
import numpy as np

D_MODEL = 1024
BATCH = 4
SEQ = 4096
DEPTH = 2
DEC_BATCH = 16
DEC_SEQ = 64
PAST_LEN = 4096

CHUNK = 64
C_CONV = D_MODEL // 2
CONV_WIDTH = 31
HEAD_DIM = 64
N_HEADS = (D_MODEL // 2) // HEAD_DIM
N_KV_HEADS = 2
GROUP = N_HEADS // N_KV_HEADS
ATTN_WIDTH = N_HEADS * HEAD_DIM
MIX_WIDTH = C_CONV + ATTN_WIDTH
N_IDX_HEADS = 8
IDX_DIM = 64
TOPK_MAX = 256
ROPE_THETA = 10000.0
QUERY_BLOCK = 128
D_FF = -(-8 * D_MODEL // (3 * 256)) * 256
IN_DIM = 2 * C_CONV + (N_HEADS + 2 * N_KV_HEADS) * HEAD_DIM + N_IDX_HEADS * IDX_DIM + IDX_DIM + N_IDX_HEADS
ALPHA = (2 * DEPTH) ** 0.25
BETA = (8 * DEPTH) ** -0.25
ATTN_SCALE = HEAD_DIM ** -0.5
IDX_SCALE = IDX_DIM ** -0.5
IDX_HEAD_SCALE = N_IDX_HEADS ** -0.5
LN_EPS = 1e-5

kernel_name = "hybrid_conv_dsa_stream_step"


def layer_norm(x, g, b):
    xf = x.astype(jnp.float32)
    mu = jnp.mean(xf, axis=-1, keepdims=True)
    var = jnp.mean(jnp.square(xf - mu), axis=-1, keepdims=True)
    y = (xf - mu) * lax.rsqrt(var + LN_EPS) * g.astype(jnp.float32) + b.astype(jnp.float32)
    return y.astype(x.dtype)


def rope(x, pos):
    half = x.shape[-1] // 2
    inv = ROPE_THETA ** (-jnp.arange(half, dtype=jnp.float32) / half)
    ang = pos.astype(jnp.float32)[:, None] * inv[None, :]
    cos = jnp.cos(ang)[None, :, None, :]
    sin = jnp.sin(ang)[None, :, None, :]
    xf = x.astype(jnp.float32)
    x1, x2 = xf[..., :half], xf[..., half:]
    return jnp.concatenate([x1 * cos - x2 * sin, x2 * cos + x1 * sin], axis=-1).astype(x.dtype)


def conv_mixer(u, conv_state, conv_w, conv_b, ln_g, ln_b):
    full = jnp.concatenate([conv_state.astype(u.dtype), u], axis=1)
    y = lax.conv_general_dilated(full, conv_w[:, None, :].astype(u.dtype), window_strides=(1,),
                                 padding='VALID', dimension_numbers=('NWC', 'WIO', 'NWC'),
                                 feature_group_count=C_CONV) + conv_b
    y = jax.nn.silu(layer_norm(y, ln_g, ln_b))
    return y, full[:, -(CONV_WIDTH - 1):]


def dsa_attention(q, k, v, qi, ki, wi, q_pos, topk):
    B, T = q.shape[0], q.shape[1]
    S = k.shape[1]
    qb = min(QUERY_BLOCK, T)
    nb = T // qb
    key_chunk = jnp.arange(S, dtype=jnp.int32) // CHUNK
    ki32 = ki.astype(jnp.float32)
    bidx = jnp.arange(B)[:, None, None]

    def block(args):
        q_b, qi_b, wi_b, pos_b = args
        q_chunk = pos_b // CHUNK
        admit = key_chunk[None, :] <= q_chunk[:, None]
        dots = jnp.einsum('bqhd,bsd->bqhs', qi_b.astype(jnp.float32), ki32) * IDX_SCALE
        score = jnp.einsum('bqh,bqhs->bqs', wi_b.astype(jnp.float32), jax.nn.relu(dots))
        score = jnp.where(admit[None], score, -jnp.inf)
        _, idx = lax.top_k(score, topk)
        valid = (idx // CHUNK) <= q_chunk[None, :, None]
        kg = k[bidx, idx].astype(jnp.float32)
        vg = v[bidx, idx].astype(jnp.float32)
        qg = q_b.reshape(B, qb, N_KV_HEADS, GROUP, HEAD_DIM).astype(jnp.float32)
        logits = jnp.einsum('bqgrd,bqkgd->bqgrk', qg, kg) * ATTN_SCALE
        logits = jnp.where(valid[:, :, None, None, :], logits, -jnp.inf)
        p = jax.nn.softmax(logits, axis=-1)
        o = jnp.einsum('bqgrk,bqkgd->bqgrd', p, vg)
        return o.reshape(B, qb, ATTN_WIDTH).astype(q.dtype)

    def split(a):
        return jnp.moveaxis(a.reshape((B, nb, qb) + a.shape[2:]), 1, 0)

    out = lax.map(block, (split(q), split(qi), split(wi), q_pos.reshape(nb, qb)))
    return jnp.moveaxis(out, 0, 1).reshape(B, T, ATTN_WIDTH)


def trunk_layer(x, conv_state, k_past, v_past, ki_past,
                w_in, conv_w, conv_b, cln_g, cln_b, w_o, ln1_g, ln1_b,
                w_gate_up, w_down, ln2_g, ln2_b):
    B, T, _ = x.shape
    past = k_past.shape[1]
    pos = jnp.arange(T, dtype=jnp.int32) + past
    sizes = (C_CONV, C_CONV, ATTN_WIDTH, N_KV_HEADS * HEAD_DIM, N_KV_HEADS * HEAD_DIM,
             N_IDX_HEADS * IDX_DIM, IDX_DIM, N_IDX_HEADS)
    points = [sum(sizes[:i + 1]) for i in range(len(sizes) - 1)]
    h = x @ w_in
    a, g, q, k, v, qi, ki, wi = jnp.split(h, points, axis=-1)
    u = a * jax.nn.sigmoid(g)
    conv_out, new_conv = conv_mixer(u, conv_state, conv_w, conv_b, cln_g, cln_b)
    q = rope(q.reshape(B, T, N_HEADS, HEAD_DIM), pos)
    k = rope(k.reshape(B, T, N_KV_HEADS, HEAD_DIM), pos)
    v = v.reshape(B, T, N_KV_HEADS, HEAD_DIM)
    qi = rope(qi.reshape(B, T, N_IDX_HEADS, IDX_DIM), pos)
    ki = rope(ki[:, :, None, :], pos)[:, :, 0]
    wi = wi * IDX_HEAD_SCALE
    k_all = jnp.concatenate([k_past.astype(k.dtype), k], axis=1)
    v_all = jnp.concatenate([v_past.astype(v.dtype), v], axis=1)
    ki_all = jnp.concatenate([ki_past.astype(ki.dtype), ki], axis=1)
    topk = min(TOPK_MAX, k_all.shape[1] // 4)
    attn_out = dsa_attention(q, k_all, v_all, qi, ki_all, wi, pos, topk)
    mix = jnp.concatenate([conv_out, attn_out], axis=-1) @ w_o
    x = layer_norm(ALPHA * x + mix, ln1_g, ln1_b)
    gate, up = jnp.split(x @ w_gate_up, 2, axis=-1)
    ffn = (jax.nn.silu(gate) * up) @ w_down
    x = layer_norm(ALPHA * x + ffn, ln2_g, ln2_b)
    return x, k, v, ki, new_conv


def setup_inputs(seed: int = 0) -> dict:
    key = jax.random.key(seed)
    ks = jax.random.split(key, 20)

    def nrm(k, shape, s):
        return jax.random.normal(k, shape, jnp.float32) * s

    return {
        "x_prompt": nrm(ks[0], (BATCH, SEQ, D_MODEL), 1.0),
        "x_sample": nrm(ks[1], (DEC_BATCH, DEC_SEQ, D_MODEL), 1.0),
        "cache_k": nrm(ks[2], (DEPTH, DEC_BATCH, PAST_LEN, N_KV_HEADS, HEAD_DIM), 1.0),
        "cache_v": nrm(ks[3], (DEPTH, DEC_BATCH, PAST_LEN, N_KV_HEADS, HEAD_DIM), 1.0),
        "cache_k_idx": nrm(ks[4], (DEPTH, DEC_BATCH, PAST_LEN, IDX_DIM), 1.0),
        "state_conv": nrm(ks[5], (DEPTH, DEC_BATCH, CONV_WIDTH - 1, C_CONV), 0.5),
        "w_in": nrm(ks[6], (DEPTH, D_MODEL, IN_DIM), D_MODEL ** -0.5),
        "conv_w": nrm(ks[7], (DEPTH, CONV_WIDTH, C_CONV), CONV_WIDTH ** -0.5),
        "conv_b": nrm(ks[8], (DEPTH, C_CONV), 0.02),
        "conv_ln_g": 1.0 + nrm(ks[9], (DEPTH, C_CONV), 0.02),
        "conv_ln_b": nrm(ks[10], (DEPTH, C_CONV), 0.02),
        "w_o": nrm(ks[11], (DEPTH, MIX_WIDTH, D_MODEL), MIX_WIDTH ** -0.5 * BETA),
        "ln1_g": 1.0 + nrm(ks[12], (DEPTH, D_MODEL), 0.02),
        "ln1_b": nrm(ks[13], (DEPTH, D_MODEL), 0.02),
        "w_gate_up": nrm(ks[14], (DEPTH, D_MODEL, 2 * D_FF), D_MODEL ** -0.5),
        "w_down": nrm(ks[15], (DEPTH, D_FF, D_MODEL), D_FF ** -0.5 * BETA),
        "ln2_g": 1.0 + nrm(ks[16], (DEPTH, D_MODEL), 0.02),
        "ln2_b": nrm(ks[17], (DEPTH, D_MODEL), 0.02),
    }


def reference(x_prompt, x_sample, cache_k, cache_v, cache_k_idx, state_conv,
              w_in, conv_w, conv_b, conv_ln_g, conv_ln_b, w_o, ln1_g, ln1_b,
              w_gate_up, w_down, ln2_g, ln2_b):
    Bp = x_prompt.shape[0]
    dt = x_prompt.dtype
    empty_kv = jnp.zeros((Bp, 0, N_KV_HEADS, HEAD_DIM), dt)
    empty_ki = jnp.zeros((Bp, 0, IDX_DIM), dt)
    zero_conv = jnp.zeros((Bp, CONV_WIDTH - 1, C_CONV), dt)
    hp, hs = x_prompt, x_sample
    kp, vp, kip, cp = [], [], [], []
    ksl, vsl, kisl, csl = [], [], [], []
    for l in range(DEPTH):
        w = (w_in[l], conv_w[l], conv_b[l], conv_ln_g[l], conv_ln_b[l], w_o[l],
             ln1_g[l], ln1_b[l], w_gate_up[l], w_down[l], ln2_g[l], ln2_b[l])
        hp, k1, v1, ki1, c1 = trunk_layer(hp, zero_conv, empty_kv, empty_kv, empty_ki, *w)
        hs, k2, v2, ki2, c2 = trunk_layer(hs, state_conv[l], cache_k[l], cache_v[l], cache_k_idx[l], *w)
        kp.append(k1); vp.append(v1); kip.append(ki1); cp.append(c1)
        ksl.append(k2); vsl.append(v2); kisl.append(ki2); csl.append(c2)
    return (hp, hs,
            jnp.stack(kp), jnp.stack(vp), jnp.stack(kip), jnp.stack(cp),
            jnp.stack(ksl), jnp.stack(vsl), jnp.stack(kisl), jnp.stack(csl))
```

```python
import contextlib
import numpy as np
import concourse.bass as bass
import concourse.mybir as mybir
from concourse.bass_utils import run_bass_kernel_spmd

F32 = mybir.dt.float32
BF16 = mybir.dt.bfloat16
AF = mybir.ActivationFunctionType
ALU = mybir.AluOpType
AX = mybir.AxisListType

D = 1024
DEPTH = 2
SEQ = 4096
NPT = 16
NT = 17
TOK = NT * 128
CONVW = 31
CC = 512
DFF = 2816
NFC = 22
ALPHA = float((2 * DEPTH) ** 0.25)
ATTN_SCALE = 64 ** -0.5
WI_SCALE = float((64 ** -0.5) * (8 ** -0.5))
LN_EPS = 1e-5
TOPK = 256
NBIS = 20
NEG = -1.0e30
NPRM = 168
EXW = 10240

ENGS = ("pe", "act", "dve", "pool", "sp")
SEM_EPOCH = 30000
N_EPOCH = 3
DMA_POOL = 20
RELAX_SAME_ENGINE = False
PE_ACC_NOWAIT = True

import os
LAST_PHASE = int(os.environ.get("K_LAST_PHASE", "99"))
NO_CC = False
A_MODE = int(os.environ.get("K_A_MODE", "255"))
F_MODE = int(os.environ.get("K_F_MODE", "255"))
LN_STEPS = 99


class Buf:
    __slots__ = ("name", "w", "r")

    def __init__(self, name=""):
        self.name = name
        self.w = None
        self.r = {}


class _PeProbe:
    start = True

    def matmul(self, *a, **kw):
        self.start = kw.get("start", True)
        return self

    def transpose(self, *a, **kw):
        self.start = True
        return self


class Prog:
    def __init__(self, nc, st):
        self.nc = nc
        self.ops = {e: [] for e in ENGS}
        self.cnt = {e: 0 for e in ENGS}
        self.waited = {e: {} for e in ENGS}
        self.dma_rr = {e: 0 for e in ENGS}
        self.dma_val = {}
        self.sems = {}
        i = 0
        for e in ENGS:
            for ep in range(N_EPOCH):
                self.sems[("E", e, ep)] = st.enter_context(nc.semaphore("s%d" % i)); i += 1
        for e in ("sp", "pool", "act"):
            for j in range(DMA_POOL):
                self.sems[("D", e, j)] = st.enter_context(nc.semaphore("s%d" % i)); i += 1
        self.n_instr = 0
        self.dummy = None
        self.last_raw = {}
        self.defer = None

    def replay(self, item):
        kind, eng, fn, reads, writes, inc = item
        if kind == "op":
            self.op(eng, fn, reads, writes)
        else:
            self.dma(eng, fn, reads, writes, inc)

    def _wait(self, eng, ev):
        if ev is None:
            return
        k, v = ev
        if self.waited[eng].get(k, 0) >= v:
            return
        if getattr(self, "pe_cont", False) and k[0] == "E" and k[1] == "pe" and eng == "pe":
            return
        if RELAX_SAME_ENGINE and k[0] == "E" and k[1] == eng:
            cur_ep = self.cnt[eng] // SEM_EPOCH
            if eng in ("act", "dve"):
                prod = k[2] * SEM_EPOCH + v
                if self.cnt[eng] - prod >= 1:
                    return
                if eng == "dve" and self.dummy is not None and self.last_raw.get(eng) != self.cnt[eng]:
                    d = self.dummy
                    self.ops[eng].append(("raw", lambda e: e.memset(d, 0.0)))
                    self.last_raw[eng] = self.cnt[eng]
                    return
                if eng == "dve" and self.dummy is not None:
                    return
        self.waited[eng][k] = v
        self.ops[eng].append(("wait", k, v))

    def _deps(self, eng, reads, writes):
        for b in reads:
            self._wait(eng, b.w)
        for b in writes:
            self._wait(eng, b.w)
            for ev in b.r.items():
                self._wait(eng, ev)

    def _commit(self, ev, reads, writes):
        for b in writes:
            b.w = ev
            b.r = {}
        for b in reads:
            if b in writes:
                continue
            if b.r.get(ev[0], 0) < ev[1]:
                b.r[ev[0]] = ev[1]

    def op(self, eng, fn, reads=(), writes=()):
        if self.defer is not None:
            self.defer.append(("op", eng, fn, tuple(reads), tuple(writes), 0))
            return
        self.pe_cont = False
        if eng == "pe" and PE_ACC_NOWAIT:
            pr = _PeProbe()
            fn(pr)
            self.pe_cont = (pr.start is False)
        self._deps(eng, reads, writes)
        self.pe_cont = False
        self.cnt[eng] += 1
        c = self.cnt[eng]
        k = ("E", eng, (c - 1) // SEM_EPOCH)
        v = (c - 1) % SEM_EPOCH + 1
        self.ops[eng].append(("op", fn, k, 1))
        self._commit((k, v), reads, writes)
        self.n_instr += 1

    def dma(self, eng, fn, reads=(), writes=(), inc=16):
        if self.defer is not None:
            self.defer.append(("dma", eng, fn, tuple(reads), tuple(writes), inc))
            return
        i = self.dma_rr[eng] % DMA_POOL
        self.dma_rr[eng] += 1
        k = ("D", eng, i)
        prev = self.dma_val.get(k, 0)
        if prev:
            self._wait(eng, (k, prev))
        self._deps(eng, reads, writes)
        v = prev + inc
        self.dma_val[k] = v
        self.ops[eng].append(("op", fn, k, inc))
        self._commit((k, v), reads, writes)
        self.n_instr += 1

    def barrier(self):
        evs = []
        for e in ENGS:
            c = self.cnt[e]
            if c:
                evs.append((("E", e, (c - 1) // SEM_EPOCH), (c - 1) % SEM_EPOCH + 1))
        for k, v in self.dma_val.items():
            evs.append((k, v))
        for e in ENGS:
            for ev in evs:
                self._wait(e, ev)

    def emit(self):
        nc = self.nc
        ops = self.ops
        sems = self.sems

        def run(e, lst):
            for it in lst:
                if it[0] == "wait":
                    e.wait_ge(sems[it[1]], it[2])
                elif it[0] == "raw":
                    it[1](e)
                else:
                    it[1](e).then_inc(sems[it[2]], it[3])

        with nc.Block() as block:
            @block.tensor
            def _(e):
                run(e, ops["pe"])

            @block.scalar
            def _(e):
                run(e, ops["act"])

            @block.vector
            def _(e):
                run(e, ops["dve"])

            @block.gpsimd
            def _(e):
                run(e, ops["pool"])

            @block.sync
            def _(e):
                run(e, ops["sp"])
        self.ops = {e: [] for e in ENGS}


class Ring:
    def __init__(self, items):
        self.items = items
        self.i = 0

    def next(self):
        it = self.items[self.i % len(self.items)]
        self.i += 1
        return it


def build_program():
    nc = bass.Bass("TRN2", target_bir_lowering=False)

    def din(name, shape, dt=F32):
        return nc.dram_tensor(name, shape, dt, kind="ExternalInput")

    def dout(name, shape, dt=F32):
        return nc.dram_tensor(name, shape, dt, kind="ExternalOutput")

    x_own = din("x_own", [TOK, D])
    rope_d = din("rope", [128, NT, 96])
    meo_d = din("meo", [128, 256])
    flags_d = din("flags", [128, 2])
    ck_d = din("ck", [DEPTH, 2, SEQ, 128])
    cv_d = din("cv", [DEPTH, 2, SEQ, 128])
    cki_d = din("cki", [DEPTH, 2, SEQ, 64])
    sconv_d = din("sconv", [DEPTH, 2, 30, CC])
    w_in_d = din("w_in", [DEPTH, D, 2376])
    prm_d = din("prm", [DEPTH, 128, NPRM])
    w_o_d = din("w_o", [DEPTH, D, D])
    w_gu_d = din("w_gu", [DEPTH, D, 2 * DFF])
    w_dn_d = din("w_dn", [DEPTH, DFF, D])

    y_d = dout("y", [TOK, D])
    nk_d = dout("nk", [DEPTH, TOK, 128])
    nv_d = dout("nv", [DEPTH, TOK, 128])
    nki_d = dout("nki", [DEPTH, TOK, 64])
    ncv_d = dout("ncv", [DEPTH, 3, 30, CC])

    XT = [nc.dram_tensor("xt%d" % l, [128, 8, TOK], F32) for l in range(DEPTH)]
    XN = [nc.dram_tensor("xn%d" % l, [128, 8, TOK], F32) for l in range(DEPTH)]
    EXP = ((0, 4096), (4096, 4096), (8192, 2048))
    IB = [[nc.dram_tensor("ib%d_%d" % (l, k), [128, w_], BF16) for k, (c_, w_) in enumerate(EXP)] for l in range(DEPTH)]
    OB = [[nc.dram_tensor("ob%d_%d" % (l, k), [256, w_], BF16) for k, (c_, w_) in enumerate(EXP)] for l in range(DEPTH)]

    def ob_ap(l, r, col, n):
        for k, (c_, w_) in enumerate(EXP):
            if c_ <= col and col + n <= c_ + w_:
                return OB[l][k][r * 128:(r + 1) * 128, col - c_:col - c_ + n]
        raise AssertionError
    B_XT = [[Buf() for _ in range(NT)] for _ in range(DEPTH)]
    B_XN = [[Buf() for _ in range(NT)] for _ in range(DEPTH)]
    B_IB = [Buf() for _ in range(DEPTH)]
    B_OB = [Buf() for _ in range(DEPTH)]

    phase_no = [0]

    with contextlib.ExitStack() as gst:
        P = Prog(nc, gst)

        uid = [0]

        def sbuf(st, name, shape, dt):
            uid[0] += 1
            return st.enter_context(nc.sbuf_tensor("%s_%d" % (name, uid[0]), shape, dt))

        def psum(st, name, shape, dt):
            uid[0] += 1
            return st.enter_context(nc.psum_tensor("%s_%d" % (name, uid[0]), shape, dt))

        ident_f = sbuf(gst, "ident_f", [128, 128], F32)
        ident_b = sbuf(gst, "ident_b", [128, 128], BF16)
        ones_f = sbuf(gst, "ones_f", [128, 128], F32)
        sel_f = sbuf(gst, "sel_f", [65, 64], F32)
        meo_s = sbuf(gst, "meo_s", [128, 256], F32)
        flags_s = sbuf(gst, "flags_s", [128, 2], F32)
        prm_s = sbuf(gst, "prm_s", [128, DEPTH, NPRM], F32)
        B_const = Buf("const")
        CK = sbuf(gst, "CK", [128, NBIS + 1], F32)
        dummy_t = sbuf(gst, "dummy_t", [128, 8], F32)
        P.dummy = None

        P.op("pool", lambda e: e.memset(ident_f[:], 0.0), writes=[B_const])
        P.op("pool", lambda e: e.affine_select(out=ident_f[:], in_=ident_f[:], pattern=[[-1, 128]],
                                               compare_op=ALU.not_equal, fill=1.0, base=0,
                                               channel_multiplier=1), writes=[B_const])
        P.op("pool", lambda e: e.tensor_copy(out=ident_b[:], in_=ident_f[:]), writes=[B_const])
        P.op("pool", lambda e: e.memset(ones_f[:], 1.0), writes=[B_const])
        for k_ in range(NBIS + 1):
            P.op("pool", lambda e, k_=k_: e.memset(CK[:, k_:k_ + 1], float(2.0 ** -k_)), writes=[B_const])
        P.op("pool", lambda e: e.memset(sel_f[:], 0.0), writes=[B_const])
        P.op("pool", lambda e: e.memset(sel_f[64:65, :], 1.0), writes=[B_const])
        P.dma("sp", lambda e: e.dma_start(out=meo_s[:], in_=meo_d[:, :]), writes=[B_const])
        P.dma("sp", lambda e: e.dma_start(out=flags_s[:], in_=flags_d[:, :]), writes=[B_const])
        for l in range(DEPTH):
            P.dma("sp", lambda e, l=l: e.dma_start(out=prm_s[:, l, :], in_=prm_d[l, :, :]), writes=[B_const])

        def prm(l, off, n=1):
            return prm_s[:, l, off:off + n]
        O_CW, O_CB, O_CG, O_CBB, O_L1G, O_L1B, O_L2G, O_L2B = 0, 124, 128, 132, 136, 144, 152, 160

        def end_phase():
            P.barrier()
            P.emit()
            phase_no[0] += 1
            return phase_no[0] >= LAST_PHASE

        with contextlib.ExitStack() as st:
            xin = [sbuf(st, "xin%d" % i, [128, D], F32) for i in range(2)]
            xo = [sbuf(st, "xo%d" % i, [128, 8, 128], F32) for i in range(2)]
            pX = [psum(st, "pX%d" % i, [128, 8, 128], F32) for i in range(2)]
            B_xin = [Buf() for _ in range(2)]
            B_xo = [Buf() for _ in range(2)]
            B_pX = [Buf() for _ in range(2)]
            B_pXb = [[Buf(), Buf()] for _ in range(2)]
            for t in range(NT):
                i = t % 2
                P.dma("sp", lambda e, t=t, i=i: e.dma_start(out=xin[i][:], in_=x_own[t * 128:(t + 1) * 128, :]),
                      writes=[B_xin[i]])
                for c in (0, 4, 1, 5, 2, 6, 3, 7):
                    P.op("pe", lambda e, c=c, i=i: e.transpose(out=pX[i][:, c, :], in_=xin[i][:, c * 128:(c + 1) * 128],
                                                               identity=ident_f[:]),
                         reads=[B_xin[i], B_const], writes=[B_pXb[i][c // 4]])
                P.op("act", lambda e, i=i: e.activation(out=xo[i][:], in_=pX[i][:], func=AF.Copy),
                     reads=B_pXb[i], writes=[B_xo[i]])
                P.dma("sp", lambda e, t=t, i=i: e.dma_start(out=XT[0][:, :, t * 128:(t + 1) * 128], in_=xo[i][:]),
                      reads=[B_xo[i]], writes=[B_XT[0][t]])
            stop = end_phase()

        for l in range(DEPTH):
            if stop:
                break
            with contextlib.ExitStack() as pa:
                QT = sbuf(pa, "QT", [128, NT, 512], BF16)
                QIT = sbuf(pa, "QIT", [128, NT, 512], BF16)
                UT = sbuf(pa, "UT", [128, 4, NPT * 160 + 2 * 96], BF16)
                WI = sbuf(pa, "WI", [128, NT + 1, 8], F32)
                SNK = sbuf(pa, "SNK", [128, 3, 128], BF16)
                SNV = sbuf(pa, "SNV", [64, 2, 128], BF16)
                B_QT = [Buf() for _ in range(NT)]
                B_QIT = [Buf() for _ in range(NT)]
                B_UT = Buf()
                B_UTh = Buf()
                B_WI = Buf()
                B_SN = Buf()

                def useg(t):
                    if t < NPT:
                        return [(t * 160, 128, 0)]
                    return [(NPT * 160, 64, 0), (NPT * 160 + 96, 64, 64)]

                with contextlib.ExitStack() as st:
                    W = sbuf(st, "Win", [128, 8, 2376], BF16)
                    B_W = Buf()
                    rope_s = sbuf(st, "rope_s", [128, NT, 96], F32)
                    B_rope = Buf()
                    P.dma("sp", lambda e: e.dma_start(out=rope_s[:], in_=rope_d[:, :, :]), writes=[B_rope])
                    EXS = sbuf(st, "EXS", [128, EXW], BF16)
                    B_EXS = Buf()
                    xTb = [sbuf(st, "xTb%d" % i, [128, 8, 128], BF16) for i in range(3)]
                    B_xTb = [Buf() for _ in range(3)]
                    PT = [sbuf(st, "PT%d" % i, [128, 1352], F32) for i in range(2)]
                    B_PT = [Buf() for _ in range(2)]
                    T1 = sbuf(st, "T1", [128, 1216], F32)
                    T2 = sbuf(st, "T2", [128, 1216], F32)
                    RT = [sbuf(st, "RT%d" % i, [128, 1216], F32) for i in range(2)]
                    B_T1, B_T2 = Buf(), Buf()
                    B_RT = [Buf() for _ in range(2)]
                    TB = [sbuf(st, "TB%d" % i, [128, 1408], BF16) for i in range(2)]
                    B_TB = [Buf() for _ in range(2)]
                    SG = sbuf(st, "SG", [128, 4, 128], F32)
                    U = [sbuf(st, "U%d" % i, [128, 4, 128], F32) for i in range(2)]
                    B_SG = Buf()
                    B_U = [Buf() for _ in range(2)]
                    NCS = sbuf(st, "NCS", [128, 512], F32)
                    B_NCS = Buf()
                    PV2 = sbuf(st, "PV2", [64, 136], F32)
                    B_PV2 = Buf()
                    pA = psum(st, "pA", [128, 512], F32)
                    pB = psum(st, "pB", [128, 512], F32)
                    pC = psum(st, "pC", [128, 512], F32)
                    pAG = psum(st, "pAG", [128, 8, 128], F32)
                    pTr = psum(st, "pTr", [128, 2048], BF16)
                    pV2 = psum(st, "pV2", [128, 512], F32)
                    B_pA, B_pB, B_pC, B_pAG, B_pTr, B_pV2 = Buf(), Buf(), Buf(), Buf(), Buf(), Buf()
                    B_pAGb = [Buf(), Buf()]
                    B_pTrb = [Buf(), Buf()]

                    print("phase P sbuf remaining", nc.sbuf_bytes_remaining)
                    def load_x(t_):
                        P.dma("pool", lambda e: e.dma_start(out=xTb[t_ % 3][:], in_=XT[l][:, :, t_ * 128:(t_ + 1) * 128]),
                              reads=[B_XT[l][t_]], writes=[B_xTb[t_ % 3]])

                    load_x(0)
                    load_x(1)
                    wsrc = w_in_d.ap()[l].rearrange("(c p) n -> p c n", p=128)
                    for (s0, s1, d0) in ((0, 1536, 0), (1792, 2304, 1536), (1536, 1664, 2048), (2304, 2368, 2176),
                                         (1664, 1792, 2240), (2368, 2376, 2368)):
                        P.dma("pool", lambda e, s0=s0, s1=s1, d0=d0: e.dma_start(out=W[:, :, d0:d0 + (s1 - s0)],
                                                                                  in_=wsrc[:, :, s0:s1]), writes=[B_W])
                    P.op("pool", lambda e: e.memset(EXS[:], 0.0), writes=[B_EXS])
                    P.op("pool", lambda e: e.memset(UT[:], 0.0), writes=[B_UT, B_UTh])

                    def p_stage1(t):
                        i = t % 2
                        if t + 2 < NT:
                            load_x(t + 2)
                        for (pp, Bp, c0, nn) in ((pA, B_pA, 1024, 512), (pB, B_pB, 1536, 512), (pC, B_pC, 2048, 328)):
                            for c in range(8):
                                P.op("pe", lambda e, pp=pp, c=c, c0=c0, nn=nn, i=i: e.matmul(
                                    pp[:, 0:nn], lhsT=xTb[t % 3][:, c, :], rhs=W[:, c, c0:c0 + nn], start=(c == 0), stop=(c == 7)),
                                    reads=[B_xTb[t % 3], B_W], writes=[Bp])
                        P.op("act", lambda e, i=i: e.activation(out=PT[i][:, 0:512], in_=pA[:, 0:512], func=AF.Copy),
                             reads=[B_pA], writes=[B_PT[i]])
                        P.op("act", lambda e, i=i: e.activation(out=PT[i][:, 512:1024], in_=pB[:, 0:512], func=AF.Copy),
                             reads=[B_pB], writes=[B_PT[i]])
                        P.op("act", lambda e, i=i: e.activation(out=PT[i][:, 1024:1352], in_=pC[:, 0:328], func=AF.Copy),
                             reads=[B_pC], writes=[B_PT[i]])
                        for f in (0, 4, 1, 5, 2, 6, 3, 7):
                            for c in range(8):
                                P.op("pe", lambda e, f=f, c=c, i=i: e.matmul(
                                    pAG[:, f, :], lhsT=W[:, c, f * 128:(f + 1) * 128], rhs=xTb[t % 3][:, c, :],
                                    start=(c == 0), stop=(c == 7)), reads=[B_xTb[t % 3], B_W], writes=[B_pAGb[f // 4]])
                        if t == NT - 1:
                            for c in range(8):
                                P.op("pe", lambda e, c=c, i=i: e.matmul(
                                    pV2[0:64, 0:136], lhsT=xTb[t % 3][:, c, 64:128], rhs=W[:, c, 2240:2376],
                                    start=(c == 0), stop=(c == 7)), reads=[B_xTb[t % 3], B_W], writes=[B_pV2])
                            P.op("act", lambda e: e.activation(out=PV2[:], in_=pV2[0:64, 0:136], func=AF.Copy),
                                 reads=[B_pV2], writes=[B_PV2])

                    def p_stage1b(t):
                        i = t % 2
                        P.op("act", lambda e: e.activation(out=SG[:], in_=pAG[:, 4:8, :], func=AF.Sigmoid),
                             reads=[B_pAGb[1]], writes=[B_SG])
                        P.op("dve", lambda e, i=i: e.tensor_tensor(out=U[i][:], in0=pAG[:, 0:4, :], in1=SG[:], op=ALU.mult),
                             reads=[B_pAGb[0], B_SG], writes=[B_U[i]])

                    def p_stage2(t):
                        i = t % 2
                        Hv = PT[i][:, 0:1216].rearrange("p (h t d) -> p h t d", t=2, d=32)
                        T1v = T1[:].rearrange("p (h t d) -> p h t d", t=2, d=32)
                        T2v = T2[:].rearrange("p (h t d) -> p h t d", t=2, d=32)
                        cosb = rope_s[:, t, 0:32].unsqueeze(1).unsqueeze(1).broadcast_to([128, 19, 2, 32])
                        sinb = rope_s[:, t, 32:64].unsqueeze(1).broadcast_to([128, 19, 32])
                        nsinb = rope_s[:, t, 64:96].unsqueeze(1).broadcast_to([128, 19, 32])
                        P.op("dve", lambda e, Hv=Hv, T1v=T1v, cosb=cosb: e.tensor_tensor(out=T1v, in0=Hv, in1=cosb, op=ALU.mult),
                             reads=[B_PT[i], B_rope], writes=[B_T1])
                        P.op("dve", lambda e, Hv=Hv, T2v=T2v, nsinb=nsinb: e.tensor_tensor(
                            out=T2v[:, :, 0, :], in0=Hv[:, :, 1, :], in1=nsinb, op=ALU.mult),
                            reads=[B_PT[i], B_rope], writes=[B_T2])
                        P.op("dve", lambda e, Hv=Hv, T2v=T2v, sinb=sinb: e.tensor_tensor(
                            out=T2v[:, :, 1, :], in0=Hv[:, :, 0, :], in1=sinb, op=ALU.mult),
                            reads=[B_PT[i], B_rope], writes=[B_T2])
                        P.op("dve", lambda e, i=i: e.tensor_tensor(out=RT[i][:], in0=T1[:], in1=T2[:], op=ALU.add),
                             reads=[B_T1, B_T2], writes=[B_RT[i]])
                        r0, r1 = t * 128, (t + 1) * 128
                        P.dma("sp", lambda e, i=i, r0=r0, r1=r1: e.dma_start(out=nk_d[l, r0:r1, :], in_=RT[i][:, 1024:1152]),
                              reads=[B_RT[i]], writes=[Buf()])
                        P.dma("sp", lambda e, i=i, r0=r0, r1=r1: e.dma_start(out=nki_d[l, r0:r1, :], in_=RT[i][:, 1152:1216]),
                              reads=[B_RT[i]], writes=[Buf()])
                        P.dma("sp", lambda e, i=i, r0=r0, r1=r1: e.dma_start(out=nv_d[l, r0:r1, :], in_=PT[i][:, 1216:1344]),
                              reads=[B_PT[i]], writes=[Buf()])
                        P.op("pool", lambda e, t=t, i=i: e.tensor_scalar(out=WI[:, t, :], in0=PT[i][:, 1344:1352], scalar1=WI_SCALE,
                                                                         scalar2=None, op0=ALU.mult), reads=[B_PT[i]], writes=[B_WI])
                        if t == NT - 1:
                            P.op("pool", lambda e: e.tensor_scalar(out=WI[0:64, NT, :], in0=PV2[:, 128:136], scalar1=WI_SCALE,
                                                                   scalar2=None, op0=ALU.mult), reads=[B_PV2], writes=[B_WI])
                            P.op("pool", lambda e, i=i: e.tensor_copy(out=SNV[:, 0, :], in_=PT[i][0:64, 1216:1344]),
                                 reads=[B_PT[i]], writes=[B_SN])
                            P.op("pool", lambda e: e.tensor_copy(out=SNV[:, 1, :], in_=PV2[:, 0:128]),
                                 reads=[B_PV2], writes=[B_SN])
                        P.op("act", lambda e, i=i: e.activation(out=TB[i][:, 0:1024], in_=RT[i][:, 0:1024], func=AF.Copy),
                             reads=[B_RT[i]], writes=[B_TB[i]])
                        kd_o = TB[i][:, 1024:1280].rearrange("p (g r d) -> p g r d", g=2, r=2)
                        kd_i = RT[i][:, 1024:1152].rearrange("p (g d) -> p g d", g=2).unsqueeze(2).broadcast_to([128, 2, 2, 64])
                        P.op("dve", lambda e, kd_o=kd_o, kd_i=kd_i: e.tensor_copy(out=kd_o, in_=kd_i),
                             reads=[B_RT[i]], writes=[B_TB[i]])
                        ki_o = TB[i][:, 1280:1408].rearrange("p (r d) -> p r d", r=2)
                        ki_i = RT[i][:, 1152:1216].unsqueeze(1).broadcast_to([128, 2, 64])
                        P.op("dve", lambda e, ki_o=ki_o, ki_i=ki_i: e.tensor_copy(out=ki_o, in_=ki_i),
                             reads=[B_RT[i]], writes=[B_TB[i]])
                        for c in (0, 8, 1, 9, 2, 10, 3, 4, 5, 6, 7):
                            P.op("pe", lambda e, c=c, i=i: e.transpose(out=pTr[:, c * 128:(c + 1) * 128],
                                                                       in_=TB[i][:, c * 128:(c + 1) * 128], identity=ident_b[:]),
                                 reads=[B_TB[i], B_const], writes=[B_pTrb[c // 8]])
                        P.op("act", lambda e, t=t: e.activation(out=QT[:, t, :], in_=pTr[:, 0:512], func=AF.Copy),
                             reads=[B_pTrb[0]], writes=[B_QT[t]])
                        P.op("act", lambda e, t=t: e.activation(out=QIT[:, t, :], in_=pTr[:, 512:1024], func=AF.Copy),
                             reads=[B_pTrb[0]], writes=[B_QIT[t]])
                        if t < NPT:
                            exo = EXS[:, 0:6144].rearrange("p (a t k) -> p a t k", a=3, k=128)[:, :, t, :]
                            P.op("act", lambda e, exo=exo: e.activation(out=exo, in_=pTr[:, 1024:1408].rearrange("p (a k) -> p a k", a=3),
                                                                        func=AF.Copy), reads=[B_pTrb[1]], writes=[B_EXS])
                            P.op("dve", lambda e, t=t, i=i: e.tensor_copy(out=EXS[:, 6144 + t * 128:6144 + (t + 1) * 128],
                                                                           in_=PT[i][:, 1216:1344]), reads=[B_PT[i]], writes=[B_EXS])
                        else:
                            P.op("act", lambda e: e.activation(out=SNK[:], in_=pTr[:, 1024:1408].rearrange("p (a k) -> p a k", a=3),
                                                               func=AF.Copy), reads=[B_pTrb[1]], writes=[B_SN])
                        for (off, n, t0) in useg(t):
                            P.op("dve", lambda e, off=off, n=n, t0=t0, i=i: e.tensor_copy(
                                out=UT[:, :, off + 32:off + 32 + n], in_=U[i][:, :, t0:t0 + n]), reads=[B_U[i]], writes=[B_UT])
                        if t < NPT:
                            tl = EXS[:, 8192:10240].rearrange("p (c t k) -> p c t k", c=4, k=32)[:, :, t, :]
                            P.op("dve", lambda e, tl=tl, i=i: e.tensor_copy(out=tl, in_=U[i][:, :, 96:128]),
                                 reads=[B_U[i]], writes=[B_EXS])
                        if t >= NPT - 1:
                            for c in range(4):
                                P.op("pe", lambda e, c=c, i=i: e.transpose(out=pA[:, c * 128:(c + 1) * 128], in_=U[i][:, c, :],
                                                                           identity=ident_f[:]),
                                     reads=[B_U[i], B_const], writes=[B_pA])
                            P.op("act", lambda e: e.activation(out=NCS[:], in_=pA[:, 0:512], func=AF.Copy),
                                 reads=[B_pA], writes=[B_NCS])
                            if t == NPT - 1:
                                P.dma("sp", lambda e: e.dma_start(out=ncv_d[l, 0, :, :], in_=NCS[98:128, :]),
                                      reads=[B_NCS], writes=[Buf()])
                            else:
                                P.dma("sp", lambda e: e.dma_start(out=ncv_d[l, 1, :, :], in_=NCS[34:64, :]),
                                      reads=[B_NCS], writes=[Buf()])
                                P.dma("sp", lambda e: e.dma_start(out=ncv_d[l, 2, :, :], in_=NCS[98:128, :]),
                                      reads=[B_NCS], writes=[Buf()])

                    for t in range(NT + 1):
                        if t < NT:
                            p_stage1(t)
                        if t >= 1:
                            p_stage2(t - 1)
                        if t < NT:
                            p_stage1b(t)
                    for k, (c_, w_) in enumerate(EXP):
                        P.dma("sp", lambda e, k=k, c_=c_, w_=w_: e.dma_start(out=IB[l][k][:, :], in_=EXS[:, c_:c_ + w_]),
                              reads=[B_EXS], writes=[B_IB[l]])
                    if not NO_CC:
                        for k in range(len(EXP)):
                            P.dma("pool", lambda e, k=k: e.collective_compute(
                                "AllGather", ALU.bypass, replica_groups=[[0, 1], [2, 3], [4, 5], [6, 7]],
                                ins=[IB[l][k].ap().opt()], outs=[OB[l][k].ap().opt()]),
                                reads=[B_IB[l]], writes=[B_OB[l]], inc=1)
                    stop = end_phase()
                if stop:
                    break

                with contextlib.ExitStack() as st:
                    NS = 33
                    K0 = sbuf(st, "K0", [128, NS * 128], BF16)
                    K1 = sbuf(st, "K1", [128, NS * 128], BF16)
                    KI = sbuf(st, "KI", [128, NS * 128], BF16)
                    VX = sbuf(st, "VX", [128, NS, 2, 65], BF16)
                    B_K = Buf()
                    B_V = Buf()
                    CKS = [sbuf(st, "CKS%d" % i, [128, 8, 128], BF16) for i in range(2)]
                    CKD = [sbuf(st, "CKD0", [128, 8, 384], BF16)] * 2
                    B_CKS = [Buf() for _ in range(2)]
                    B_CKD = [Buf()] * 2
                    CIS = [sbuf(st, "CIS%d" % i, [128, 8, 64], BF16) for i in range(2)]
                    B_CIS = [Buf() for _ in range(2)]
                    VST = [sbuf(st, "VST%d" % i, [128, 8, 128], BF16) for i in range(2)]
                    B_VST = [Buf() for _ in range(2)]
                    SCV = sbuf(st, "SCV", [32, 512], F32)
                    B_SCV = Buf()
                    SC = sbuf(st, "SC", [128, NS * 128], F32)
                    B_SC = Buf()
                    RTL = SC[:, 0:2048].bitcast(BF16).rearrange("p (r k) -> p r k", r=2)
                    B_RTL = B_SC
                    M01 = sbuf(st, "M01", [128, NS * 128], BF16)
                    B_M01 = Buf()
                    MT = [sbuf(st, "MT%d" % i, [128, 8, 128], BF16) for i in range(2)]
                    B_MT = [Buf() for _ in range(2)]
                    RB = [sbuf(st, "RB%d" % i, [128, 512], F32) for i in range(3)]
                    B_RB = [Buf() for _ in range(3)]
                    EBA = [sbuf(st, "EBA%d" % i, [128, 1024], BF16) for i in range(3)]
                    B_EBA = [Buf() for _ in range(3)]
                    PBA = [sbuf(st, "PBA%d" % i, [128, 1024], BF16) for i in range(3)]
                    B_PBA = [Buf() for _ in range(3)]
                    OS = sbuf(st, "OS", [65, 512], F32)
                    B_OS, B_RD = Buf(), Buf()
                    BS = sbuf(st, "BS", [128, 40], F32)
                    B_BS = Buf()
                    WoC = sbuf(st, "WoC", [128, 4, D], BF16)
                    WoA = sbuf(st, "WoA", [64, 8, D], BF16)
                    B_Wo = Buf()
                    ATT = sbuf(st, "ATT", [64, 8, 128], BF16)
                    B_ATT = Buf()
                    CAC = sbuf(st, "CAC", [128, 4, 128], F32)
                    B_CAC = Buf()
                    B_CACc = [Buf() for _ in range(4)]
                    CVT = sbuf(st, "CVT", [128, 4, 128], BF16)
                    B_CVT = Buf()
                    Z = sbuf(st, "Z", [128, 8, 128], F32)
                    B_Z = Buf()
                    XFc = [sbuf(st, "XFc%d" % i, [128, 128], F32) for i in range(2)]
                    B_XFc = [Buf() for _ in range(2)]
                    XNo = [sbuf(st, "XNo0", [128, 8, 128], F32)] * 2
                    B_XNo = [Buf()] * 2
                    pI = [psum(st, "pI%d" % i, [128, 512], F32) for i in range(2)]
                    pL = [psum(st, "pL%d" % i, [128, 512], F32) for i in range(2)]
                    pO = [psum(st, "pO%d" % i, [128, 512], F32) for i in range(2)]
                    pM = psum(st, "pM", [128, 1024], BF16)
                    pS = psum(st, "pS", [128, 512], F32)
                    pSb = pS[:].bitcast(BF16)
                    B_pI = [Buf() for _ in range(2)]
                    B_pL = [Buf() for _ in range(2)]
                    B_pO = [Buf() for _ in range(2)]
                    B_pM, B_pS = Buf(), Buf()
                    B_pS2 = [Buf(), Buf()]

                    wo_c = w_o_d.ap()[l][0:512, :].rearrange("(c p) n -> p c n", p=128)
                    wo_a = w_o_d.ap()[l][512:1024, :].rearrange("(h p) n -> p h n", p=64)
                    P.dma("pool", lambda e: e.dma_start(out=WoC[:], in_=wo_c), writes=[B_Wo])
                    P.dma("pool", lambda e: e.dma_start(out=WoA[:], in_=wo_a), writes=[B_Wo])
                    P.op("pool", lambda e: e.memset(VX[:], 1.0), writes=[B_V])


                    rbi = [0]
                    ebi = [0]

                    def attend(t, tok0, nq, wi_ap, ntiles, last_keys, masked, att_tok0, filler=None, bis_filler=None, part="all"):
                        S = (ntiles - 1) * 128 + last_keys
                        if part in ("all", "pre"):
                            nblk = (S + 511) // 512
                            for b in range(nblk):
                                c0 = b * 512
                                n = min(512, S - c0)
                                for h in range(8):
                                    pi = pI[h % 2]
                                    half = (h % 2) * 64
                                    ch = h // 2
                                    P.op("pe", lambda e, pi=pi, half=half, ch=ch, c0=c0, n=n: e.matmul(
                                        pi[0:nq, 0:n], lhsT=QIT[half:half + 64, t, ch * 128 + tok0:ch * 128 + tok0 + nq],
                                        rhs=KI[half:half + 64, c0:c0 + n], start=True, stop=True),
                                        reads=[B_QIT[t], B_K], writes=[B_pI[h % 2]])
                                    ri = rbi[0] % 3
                                    rbi[0] += 1
                                    P.op("act", lambda e, pi=pi, ri=ri, n=n: e.activation(out=RB[ri][0:nq, 0:n], in_=pi[0:nq, 0:n], func=AF.Relu),
                                         reads=[B_pI[h % 2]], writes=[B_RB[ri]])
                                    if h == 0:
                                        P.op("dve", lambda e, ri=ri, c0=c0, n=n: e.tensor_scalar(
                                            out=SC[0:nq, c0:c0 + n], in0=RB[ri][0:nq, 0:n], scalar1=wi_ap[:, 0:1], scalar2=None, op0=ALU.mult),
                                            reads=[B_RB[ri], B_WI], writes=[B_SC])
                                    else:
                                        P.op("dve", lambda e, ri=ri, c0=c0, n=n, h=h: e.scalar_tensor_tensor(
                                            out=SC[0:nq, c0:c0 + n], in0=RB[ri][0:nq, 0:n], scalar=wi_ap[:, h:h + 1], in1=SC[0:nq, c0:c0 + n],
                                            op0=ALU.mult, op1=ALU.add), reads=[B_RB[ri], B_WI, B_SC], writes=[B_SC])
                            P.op("dve", lambda e: e.tensor_reduce(out=BS[0:nq, 0:1], in_=SC[0:nq, 0:S], axis=AX.X, op=ALU.max,
                                                                   apply_absolute_value=True), reads=[B_SC], writes=[B_BS])
                            if masked:
                                P.op("dve", lambda e: e.tensor_tensor(out=SC[0:nq, S - 256:S], in0=SC[0:nq, S - 256:S], in1=meo_s[0:nq, :],
                                                                        op=ALU.add), reads=[B_SC, B_const], writes=[B_SC])
                            P.op("dve", lambda e: e.tensor_scalar(out=BS[0:nq, 0:1], in0=BS[0:nq, 0:1], scalar1=1.0001, scalar2=1e-6,
                                                                   op0=ALU.mult, op1=ALU.add), reads=[B_BS], writes=[B_BS])
                            P.op("dve", lambda e: e.tensor_scalar(out=BS[0:nq, 9:10 + NBIS], in0=CK[0:nq, :], scalar1=BS[0:nq, 0:1],
                                                                   scalar2=None, op0=ALU.mult), reads=[B_BS, B_const], writes=[B_BS])
                            P.op("dve", lambda e: e.memset(BS[0:nq, 1:2], 0.0), reads=[B_BS], writes=[B_BS])
                        if part in ("all", "main"):
                            for k in range(1, NBIS + 1):
                                P.op("dve", lambda e: e.tensor_scalar(out=M01[0:nq, 0:S], in0=SC[0:nq, 0:S], scalar1=BS[0:nq, 1:2], scalar2=None,
                                                                       op0=ALU.is_ge, op1=ALU.add, accum_out=BS[0:nq, 2:3]),
                                     reads=[B_SC, B_BS], writes=[B_BS, B_M01])
                                nf = -(-len(bis_filler) // (NBIS + 1 - k)) if bis_filler else 0
                                for _ in range(-(-nf // 3)):
                                    if bis_filler:
                                        P.replay(bis_filler.pop(0))
                                P.op("dve", lambda e, k=k: e.scalar_tensor_tensor(out=BS[0:nq, 3:4], in0=BS[0:nq, 2:3], scalar=TOPK - 0.5,
                                                                                   in1=BS[0:nq, 8 + k:9 + k], op0=ALU.is_ge, op1=ALU.mult),
                                     reads=[B_BS], writes=[B_BS])
                                for _ in range(-(-nf // 3)):
                                    if bis_filler:
                                        P.replay(bis_filler.pop(0))
                                P.op("dve", lambda e, k=k: e.scalar_tensor_tensor(out=BS[0:nq, 1:2], in0=BS[0:nq, 3:4],
                                                                                   scalar=BS[0:nq, 9 + k:10 + k], in1=BS[0:nq, 1:2],
                                                                                   op0=ALU.subtract, op1=ALU.add),
                                     reads=[B_BS], writes=[B_BS])
                                for _ in range(max(0, nf - 2 * (-(-nf // 3)))):
                                    if bis_filler:
                                        P.replay(bis_filler.pop(0))
                            P.op("dve", lambda e: e.tensor_tensor(out=BS[0:nq, 1:2], in0=BS[0:nq, 1:2], in1=BS[0:nq, 9 + NBIS:10 + NBIS],
                                                                   op=ALU.subtract), reads=[B_BS], writes=[B_BS])
                            P.op("dve", lambda e: e.tensor_scalar(out=M01[0:nq, 0:S], in0=SC[0:nq, 0:S], scalar1=BS[0:nq, 1:2], scalar2=None,
                                                                   op0=ALU.is_ge), reads=[B_SC, B_BS], writes=[B_M01])
                            if not (A_MODE & 4):
                                return
                            LAG = 2
                            pend = {}
                            banks = [(pL[0], pL[1], B_pL[0], B_pL[1]), (pI[0], pI[1], B_pI[0], B_pI[1])]
                            W2 = 2 * nq

                            def stage_a(idx):
                                s = idx
                                mb = s // 8
                                mi = mb % 2
                                nk = last_keys if s == ntiles - 1 else 128
                                if s % 8 == 0:
                                    for s2_ in range(mb * 8, min(ntiles, mb * 8 + 8)):
                                        nk2 = last_keys if s2_ == ntiles - 1 else 128
                                        a_ = s2_ - mb * 8
                                        tgt = pM if a_ % 2 == 0 else pSb
                                        P.op("pe", lambda e, s2_=s2_, nk2=nk2, a_=a_, tgt=tgt: e.transpose(
                                            out=tgt[0:nk2, (a_ // 2) * 128:(a_ // 2) * 128 + nq], in_=M01[0:nq, s2_ * 128:s2_ * 128 + nk2],
                                            identity=ident_b[0:nq, 0:nq]), reads=[B_M01, B_const],
                                            writes=([B_pM] if a_ % 2 == 0 else [B_pS2[0], B_pS2[1]]))
                                    mtv2 = MT[mi][:].rearrange("p (a r) k -> p a r k", r=2)
                                    P.op("act", lambda e, mtv2=mtv2: e.activation(out=mtv2[:, :, 0, :], in_=pM[:, 0:512].rearrange("p (a k) -> p a k", k=128),
                                                                                  func=AF.Copy), reads=[B_pM], writes=[B_MT[mi]])
                                    if min(ntiles, mb * 8 + 8) - mb * 8 > 1:
                                        P.op("act", lambda e, mtv2=mtv2: e.activation(out=mtv2[:, :, 1, :], in_=pSb[:, 0:512].rearrange("p (a k) -> p a k", k=128),
                                                                                      func=AF.Copy), reads=[B_pS2[0], B_pS2[1]], writes=[B_MT[mi]])
                                bA, bB, B_A, B_B = banks[idx % 2]
                                for g in range(2):
                                    KG = K0 if g == 0 else K1
                                    for half in range(2):
                                        bank, B_bank = (bA, B_A) if half == 0 else (bB, B_B)
                                        if nq == 128:
                                            P.op("pe", lambda e, bank=bank, KG=KG, half=half, s=s, nk=nk, g=g: e.matmul(
                                                bank[0:nk, g * 256:(g + 1) * 256],
                                                lhsT=KG[half * 64:half * 64 + 64, s * 128:s * 128 + nk],
                                                rhs=QT[half * 64:half * 64 + 64, t, g * 256:(g + 1) * 256],
                                                start=True, stop=True), reads=[B_QT[t], B_K], writes=[B_bank])
                                        else:
                                            for cc in range(2):
                                                P.op("pe", lambda e, bank=bank, KG=KG, half=half, s=s, nk=nk, g=g, cc=cc: e.matmul(
                                                    bank[0:nk, g * W2 + cc * nq:g * W2 + (cc + 1) * nq],
                                                    lhsT=KG[half * 64:half * 64 + 64, s * 128:s * 128 + nk],
                                                    rhs=QT[half * 64:half * 64 + 64, t, (g * 2 + cc) * 128 + tok0:(g * 2 + cc) * 128 + tok0 + nq],
                                                    start=True, stop=True), reads=[B_QT[t], B_K], writes=[B_bank])
                                ei = idx % 3
                                ebv = EBA[ei][0:nk, 0:4 * W2].rearrange("p (g h k) -> p g h k", g=2, h=2)
                                for half, (bank, B_bank) in enumerate(((bA, B_A), (bB, B_B))):
                                    P.op("act", lambda e, bank=bank, half=half, ebv=ebv, nk=nk: e.activation(
                                        out=ebv[:, :, half, :], in_=bank[0:nk, 0:2 * W2].rearrange("p (g k) -> p g k", g=2),
                                        func=AF.Exp, scale=ATTN_SCALE), reads=[B_bank], writes=[B_EBA[ei]])
                                mtv = MT[mi][0:nk, s - mb * 8, 0:nq].unsqueeze(1).broadcast_to([nk, 8, nq])
                                P.op("dve", lambda e, ei=ei, nk=nk, mtv=mtv: e.tensor_tensor(
                                    out=PBA[ei][0:nk, 0:8 * nq].rearrange("p (a k) -> p a k", a=8),
                                    in0=EBA[ei][0:nk, 0:8 * nq].rearrange("p (a k) -> p a k", a=8), in1=mtv, op=ALU.mult),
                                    reads=[B_EBA[ei], B_MT[mi]], writes=[B_PBA[ei]])
                                pend[idx] = (ei, nk)

                            def stage_b(idx):
                                s = idx
                                ei, nk = pend.pop(idx)
                                for g in range(2):
                                    P.op("pe", lambda e, ei=ei, nk=nk, s=s, g=g: e.matmul(
                                        pO[g][0:65, 0:4 * nq], lhsT=VX[0:nk, s, g, :], rhs=PBA[ei][0:nk, g * 4 * nq:(g + 1) * 4 * nq],
                                        start=(s == 0), stop=(s == ntiles - 1)), reads=[B_PBA[ei], B_V], writes=[B_pO[g]])

                            nsteps = ntiles + LAG
                            for step in range(nsteps):
                                if step < ntiles:
                                    stage_a(step)
                                if step >= LAG:
                                    stage_b(step - LAG)
                                if filler:
                                    nf = -(-len(filler) // (nsteps - step))
                                    for _ in range(nf):
                                        filler.pop(0)()

                            for g in range(2):
                                P.op("act", lambda e, g=g: e.activation(out=OS[:, 0:4 * nq], in_=pO[g][0:65, 0:4 * nq], func=AF.Copy),
                                     reads=[B_pO[g]], writes=[B_OS])
                                P.op("dve", lambda e: e.reciprocal(out=OS[64:65, 0:4 * nq], in_=OS[64:65, 0:4 * nq]),
                                     reads=[B_OS], writes=[B_OS])
                                P.op("pe", lambda e: e.matmul(pS[0:64, 0:4 * nq], lhsT=sel_f[:], rhs=OS[:, 0:4 * nq], start=True, stop=True),
                                     reads=[B_OS, B_const], writes=[B_pS2[0], B_pS2[1]])
                                for a, hh in enumerate((4 * g, 4 * g + 2, 4 * g + 1, 4 * g + 3)):
                                    P.op("dve", lambda e, a=a, hh=hh: e.tensor_tensor(
                                        out=ATT[:, hh, att_tok0:att_tok0 + nq], in0=OS[0:64, a * nq:(a + 1) * nq], in1=pS[0:64, a * nq:(a + 1) * nq],
                                        op=ALU.mult), reads=[B_OS, B_pS2[0], B_pS2[1]], writes=[B_ATT])

                    def conv_ops(t):
                        lst = []
                        for k in range(CONVW):
                            for c in range(4):
                                for (off, n, t0) in useg(t):
                                    src = UT[:, c, off + 2 + k:off + 2 + k + n]
                                    if k == 0:
                                        lst.append(lambda c=c, src=src, t0=t0, n=n, k=k: P.op("dve", lambda e: e.tensor_scalar(
                                            out=CAC[:, c, t0:t0 + n], in0=src, scalar1=prm(l, O_CW + c * CONVW + k),
                                            scalar2=prm(l, O_CB + c), op0=ALU.mult, op1=ALU.add),
                                            reads=[B_UT, B_UTh, B_const], writes=[B_CACc[c], B_CAC]))
                                    else:
                                        lst.append(lambda c=c, src=src, t0=t0, n=n, k=k: P.op("dve", lambda e: e.scalar_tensor_tensor(
                                            out=CAC[:, c, t0:t0 + n], in0=src, scalar=prm(l, O_CW + c * CONVW + k),
                                            in1=CAC[:, c, t0:t0 + n], op0=ALU.mult, op1=ALU.add),
                                            reads=[B_UT, B_UTh, B_const, B_CACc[c]],
                                            writes=[B_CACc[c]] + ([B_CAC] if k == CONVW - 1 else [])))
                        return lst

                    def conv_tile(t):
                        for f_ in conv_ops(t):
                            f_()

                    def finish_tile(t):
                        def cons_conv(c, T_ap, B_T):
                            P.op("act", lambda e, c=c, T_ap=T_ap: e.activation(out=CVT[:, c, :], in_=T_ap, func=AF.Silu),
                                 reads=[B_T], writes=[B_CVT])
                        if F_MODE & 1:
                            ln_conv_run(cons_conv)
                        for dc in range(8):
                            if not (F_MODE & 2):
                                break
                            xi = dc % 2
                            P.dma("sp", lambda e, dc=dc, xi=xi: e.dma_start(out=XFc[xi][:], in_=XT[l][:, dc, t * 128:(t + 1) * 128]),
                                  reads=[B_XT[l][t]], writes=[B_XFc[xi]])
                            pw = pI[dc % 2]
                            for c in range(4):
                                P.op("pe", lambda e, pw=pw, c=c, dc=dc: e.matmul(pw[:, 0:128], lhsT=WoC[:, c, dc * 128:(dc + 1) * 128],
                                                                                 rhs=CVT[:, c, :], start=(c == 0), stop=False),
                                     reads=[B_Wo, B_CVT], writes=[B_pI[dc % 2]])
                            for h in range(8):
                                P.op("pe", lambda e, pw=pw, h=h, dc=dc: e.matmul(pw[:, 0:128], lhsT=WoA[:, h, dc * 128:(dc + 1) * 128],
                                                                                 rhs=ATT[:, h, :], start=False, stop=(h == 7)),
                                     reads=[B_Wo, B_ATT], writes=[B_pI[dc % 2]])
                            P.op("dve", lambda e, pw=pw, dc=dc, xi=xi: e.scalar_tensor_tensor(
                                out=Z[:, dc, :], in0=XFc[xi][:], scalar=ALPHA, in1=pw[:, 0:128], op0=ALU.mult, op1=ALU.add),
                                reads=[B_XFc[xi], B_pI[dc % 2]], writes=[B_Z])
                        xo_i = t % 2

                        def cons_ln1(c, T_ap, B_T):
                            P.op("dve", lambda e, c=c, T_ap=T_ap: e.tensor_copy(out=XNo[xo_i][:, c, :], in_=T_ap),
                                 reads=[B_T], writes=[B_XNo[xo_i]])
                        if F_MODE & 4:
                            ln_1_run(cons_ln1)
                        if F_MODE & 8:
                            P.dma("sp", lambda e: e.dma_start(out=XN[l][:, :, t * 128:(t + 1) * 128], in_=XNo[xo_i][:]),
                                  reads=[B_XNo[xo_i]], writes=[B_XN[l][t]])

                    def _mk(run_factory_args):
                        return None
                    def ln_conv_run(cons):
                        layernorm_cfg["lc"][0](cons)
                    def ln_1_run(cons):
                        layernorm_cfg["l1"][0](cons)
                    layernorm_cfg = {}

                    def make_ln(tag, Zt, B_Zt, nch, Dn, og, ob_):
                        SQ = [sbuf(st, "%s_sq%d" % (tag, i), [128, 128], F32) for i in range(2)]
                        B_SQ = [Buf() for _ in range(2)]
                        MEAN = sbuf(st, tag + "_mean", [128, 128], F32)
                        M2 = sbuf(st, tag + "_m2", [128, 128], F32)
                        RSTD = sbuf(st, tag + "_rstd", [128, 128], F32)
                        TT = [sbuf(st, "%s_t%d" % (tag, i), [128, 128], F32) for i in range(3)]
                        B_TT = [Buf() for _ in range(3)]
                        B_M, B_M2, B_R = Buf(), Buf(), Buf()
                        state = {"i": 0}
                        N = 128

                        def run(consume):
                            s1 = pS[:, 0:N]
                            s2 = pS[:, N:2 * N]
                            if LN_STEPS < 1:
                                return
                            for c in range(nch):
                                P.op("pe", lambda e, c=c: e.matmul(s1, lhsT=ones_f[:], rhs=Zt[:, c, :], start=(c == 0), stop=(c == nch - 1)),
                                     reads=[B_Zt, B_const], writes=[B_pS2[0]])
                            if LN_STEPS < 2:
                                return
                            for c in range(nch):
                                i = state["i"] % 2
                                state["i"] += 1
                                P.op("act", lambda e, c=c, i=i: e.activation(out=SQ[i][:], in_=Zt[:, c, :], func=AF.Square),
                                     reads=[B_Zt], writes=[B_SQ[i]])
                                P.op("pe", lambda e, c=c, i=i: e.matmul(s2, lhsT=ones_f[:], rhs=SQ[i][:], start=(c == 0), stop=(c == nch - 1)),
                                     reads=[B_SQ[i], B_const], writes=[B_pS2[1]])
                            if LN_STEPS < 3:
                                return
                            P.op("dve", lambda e: e.tensor_scalar(out=MEAN[:], in0=s1, scalar1=1.0 / Dn, scalar2=None, op0=ALU.mult),
                                 reads=[B_pS2[0], B_pS2[1]], writes=[B_M])
                            if LN_STEPS < 4:
                                return
                            P.op("dve", lambda e: e.tensor_tensor(out=M2[:], in0=MEAN[:], in1=MEAN[:], op=ALU.mult),
                                 reads=[B_M], writes=[B_M2])
                            P.op("dve", lambda e: e.scalar_tensor_tensor(out=M2[:], in0=s2, scalar=1.0 / Dn, in1=M2[:],
                                                                         op0=ALU.mult, op1=ALU.subtract),
                                 reads=[B_pS2[1], B_M2], writes=[B_M2])
                            P.op("dve", lambda e: e.tensor_scalar(out=M2[:], in0=M2[:], scalar1=0.0, scalar2=LN_EPS,
                                                                  op0=ALU.max, op1=ALU.add), reads=[B_M2], writes=[B_M2])
                            if LN_STEPS < 6:
                                return
                            P.op("act", lambda e: e.activation(out=RSTD[:], in_=M2[:], func=AF.Sqrt), reads=[B_M2], writes=[B_R])
                            P.op("dve", lambda e: e.reciprocal(out=RSTD[:], in_=RSTD[:]), reads=[B_R], writes=[B_R])
                            if LN_STEPS < 8:
                                return
                            for c in range(nch):
                                i = state["i"] % 3
                                state["i"] += 1
                                P.op("dve", lambda e, c=c, i=i: e.tensor_tensor(out=TT[i][:], in0=Zt[:, c, :], in1=MEAN[:], op=ALU.subtract),
                                     reads=[B_Zt, B_M], writes=[B_TT[i]])
                                P.op("dve", lambda e, c=c, i=i: e.tensor_tensor(out=TT[i][:], in0=TT[i][:], in1=RSTD[:], op=ALU.mult),
                                     reads=[B_R, B_TT[i]], writes=[B_TT[i]])
                                P.op("dve", lambda e, c=c, i=i: e.tensor_scalar(out=TT[i][:], in0=TT[i][:], scalar1=prm(l, og + c),
                                                                                scalar2=prm(l, ob_ + c), op0=ALU.mult, op1=ALU.add),
                                     reads=[B_TT[i], B_const], writes=[B_TT[i]])
                                consume(c, TT[i][:], B_TT[i])
                        layernorm_cfg[tag] = (run,)

                    make_ln("lc", CAC, B_CAC, 4, 512.0, O_CG, O_CBB)
                    make_ln("l1", Z, B_Z, 8, 1024.0, O_L1G, O_L1B)
                    print("phase A sbuf remaining", nc.sbuf_bytes_remaining)

                    for s_ in range(2):
                        P.dma("sp", lambda e, s_=s_: e.dma_start(out=SCV[0:30, :], in_=sconv_d[l, s_, :, :]), writes=[B_SCV])
                        for c in range(4):
                            P.op("pe", lambda e, c=c: e.transpose(out=pS[:, c * 32:c * 32 + 30], in_=SCV[0:30, c * 128:(c + 1) * 128],
                                                                  identity=ident_f[0:30, 0:30]),
                                 reads=[B_SCV, B_const], writes=[B_pS2[0], B_pS2[1]])
                        off = NPT * 160 + s_ * 96
                        P.op("act", lambda e, off=off: e.activation(out=UT[:, :, off + 2:off + 32],
                                                                    in_=pS[:, 0:128].rearrange("p (c k) -> p c k", c=4)[:, :, 0:30], func=AF.Copy),
                             reads=[B_pS2[0], B_pS2[1]], writes=[B_UTh])

                    ld = [0]

                    def load_sample_keys(s_):
                        for q4 in range(4):
                            i = ld[0] % 2
                            ld[0] += 1
                            r0 = q4 * 1024
                            P.dma("pool", lambda e, i=i, r0=r0: e.dma_start(
                                out=CKS[i][:], in_=ck_d[l, s_, r0:r0 + 1024, :].rearrange("(a p) k -> p a k", p=128)), writes=[B_CKS[i]])
                            P.dma("pool", lambda e, i=i, r0=r0: e.dma_start(
                                out=CIS[i][:], in_=cki_d[l, s_, r0:r0 + 1024, :].rearrange("(a p) k -> p a k", p=128)), writes=[B_CIS[i]])
                            P.dma("pool", lambda e, i=i, r0=r0: e.dma_start(
                                out=VST[i][:], in_=cv_d[l, s_, r0:r0 + 1024, :].rearrange("(a p) k -> p a k", p=128)), writes=[B_VST[i]])
                            kd_o = CKD[i][:, :, 0:256].rearrange("p a (g r d) -> p a g r d", g=2, r=2)
                            kd_i = CKS[i][:].rearrange("p a (g d) -> p a g d", g=2).unsqueeze(3).broadcast_to([128, 8, 2, 2, 64])
                            P.op("dve", lambda e, kd_o=kd_o, kd_i=kd_i: e.tensor_copy(out=kd_o, in_=kd_i),
                                 reads=[B_CKS[i]], writes=[B_CKD[i]])
                            ki_o = CKD[i][:, :, 256:384].rearrange("p a (r d) -> p a r d", r=2)
                            ki_i = CIS[i][:].unsqueeze(2).broadcast_to([128, 8, 2, 64])
                            P.op("dve", lambda e, ki_o=ki_o, ki_i=ki_i: e.tensor_copy(out=ki_o, in_=ki_i),
                                 reads=[B_CIS[i]], writes=[B_CKD[i]])
                            P.op("dve", lambda e, i=i, q4=q4: e.tensor_copy(out=VX[:, q4 * 8:(q4 + 1) * 8, :, 0:64],
                                                                            in_=VST[i][:].rearrange("p a (g d) -> p a g d", g=2)),
                                 reads=[B_VST[i]], writes=[B_V])
                            for a3, KD in enumerate((K0, K1, KI)):
                                for a in range(8):
                                    tgt = pM if a % 2 == 0 else pSb
                                    P.op("pe", lambda e, a=a, a3=a3, i=i, tgt=tgt: e.transpose(out=tgt[:, (a // 2) * 128:(a // 2 + 1) * 128],
                                                                                              in_=CKD[i][:, a, a3 * 128:(a3 + 1) * 128], identity=ident_b[:]),
                                         reads=[B_CKD[i], B_const], writes=([B_pM] if a % 2 == 0 else [B_pS2[0], B_pS2[1]]))
                                kdv = KD[:, q4 * 1024:(q4 + 1) * 1024].rearrange("p (a r k) -> p a r k", r=2, k=128)
                                P.op("act", lambda e, kdv=kdv: e.activation(out=kdv[:, :, 0, :], in_=pM[:, 0:512].rearrange("p (a k) -> p a k", k=128),
                                                                            func=AF.Copy), reads=[B_pM], writes=[B_K])
                                P.op("act", lambda e, kdv=kdv: e.activation(out=kdv[:, :, 1, :], in_=pSb[:, 0:512].rearrange("p (a k) -> p a k", k=128),
                                                                            func=AF.Copy), reads=[B_pS2[0], B_pS2[1]], writes=[B_K])
                        for a3, KD in enumerate((K0, K1, KI)):
                            P.op("dve", lambda e, KD=KD, a3=a3: e.tensor_copy(out=KD[:, 4096:4160], in_=SNK[:, a3, s_ * 64:s_ * 64 + 64]),
                                 reads=[B_SN], writes=[B_K])
                        P.op("dve", lambda e: e.tensor_copy(out=VX[0:64, 32, :, 0:64], in_=SNV[:, s_, :].rearrange("p (g d) -> p g d", g=2)),
                             reads=[B_SN], writes=[B_V])

                    fl16 = conv_ops(NT - 1) if (A_MODE & 16) else []
                    for s_ in range(2):
                        if not (A_MODE & 1):
                            break
                        load_sample_keys(s_)
                        wi_ap = WI[0:64, NT - 1, :] if s_ == 0 else WI[0:64, NT, :]
                        if A_MODE & 8:
                            attend(NT - 1, s_ * 64, 64, wi_ap, 33, 64, False, s_ * 64,
                                   filler=(fl16 if s_ == 1 else None))
                    for f_ in fl16:
                        f_()
                    fl16 = []
                    pending = []
                    if A_MODE & 32:
                        P.defer = []
                        finish_tile(NT - 1)
                        pending = P.defer
                        P.defer = None

                    for r in range(2):
                        for a3, KD in enumerate((K0, K1, KI)):
                            P.dma("sp", lambda e, r=r, a3=a3, KD=KD: e.dma_start(
                                out=KD[:, 0:4096].rearrange("p (j r k) -> p j r k", r=2, k=128)[:, :, r, :],
                                in_=ob_ap(l, r, a3 * 2048, 2048).rearrange("p (j k) -> p j k", k=128)),
                                reads=[B_OB[l]], writes=[B_K])
                        P.dma("sp", lambda e, r=r: e.dma_start(out=RTL[:, r, :], in_=ob_ap(l, r, 8192, 2048)),
                              reads=[B_OB[l]], writes=[B_RTL])
                        for q2 in range(2):
                            i = ld[0] % 2
                            ld[0] += 1
                            P.dma("sp", lambda e, r=r, q2=q2, i=i: e.dma_start(
                                out=VST[i][:], in_=ob_ap(l, r, 6144 + q2 * 1024, 1024).rearrange("p (a k) -> p a k", k=128)),
                                reads=[B_OB[l]], writes=[B_VST[i]])
                            P.op("dve", lambda e, r=r, q2=q2, i=i: e.tensor_copy(
                                out=VX[:, 0:32, :, 0:64].rearrange("p (j r) g d -> p j r g d", r=2)[:, q2 * 8:(q2 + 1) * 8, r, :, :],
                                in_=VST[i][:].rearrange("p a (g d) -> p a g d", g=2)), reads=[B_VST[i]], writes=[B_V])
                    r0v = RTL[:, 0, :].rearrange("p (c j k) -> p c j k", c=4, k=32)
                    r1v = RTL[:, 1, :].rearrange("p (c j k) -> p c j k", c=4, k=32)
                    uth = UT[:, :, 0:NPT * 160].rearrange("p c (j k) -> p c j k", k=160)
                    for c in range(4):
                        P.op("dve", lambda e, c=c: e.tensor_scalar(out=uth[:, c, :, 0:32], in0=r0v[:, c, :, :], scalar1=flags_s[:, 1:2],
                                                                   scalar2=None, op0=ALU.mult),
                             reads=[B_RTL, B_const], writes=[B_UTh])
                        P.op("dve", lambda e, c=c: e.scalar_tensor_tensor(out=uth[:, c, 1:NPT, 0:32], in0=r1v[:, c, 0:NPT - 1, :],
                                                                          scalar=flags_s[:, 0:1], in1=uth[:, c, 1:NPT, 0:32],
                                                                          op0=ALU.mult, op1=ALU.add),
                             reads=[B_RTL, B_const, B_UTh], writes=[B_UTh])
                    for j in range(NPT):
                        if not (A_MODE & 2):
                            break
                        fl = conv_ops(j) if (A_MODE & 16) else []
                        if A_MODE & 8:
                            if j == 0:
                                attend(j, 0, 128, WI[:, j, :], 2 * j + 2, 128, True, 0, part="pre")
                            attend(j, 0, 128, WI[:, j, :], 2 * j + 2, 128, True, 0, filler=fl, bis_filler=pending, part="main")
                            if j + 1 < NPT:
                                attend(j + 1, 0, 128, WI[:, j + 1, :], 2 * j + 4, 128, True, 0, part="pre")
                            attend(j, 0, 128, WI[:, j, :], 2 * j + 2, 128, True, 0, part="post")
                        for it_ in pending:
                            P.replay(it_)
                        pending = []
                        for f_ in fl:
                            f_()
                        if A_MODE & 32:
                            P.defer = []
                            finish_tile(j)
                            pending = P.defer
                            P.defer = None
                    for it_ in pending:
                        P.replay(it_)
                    pending = []
                    stop = end_phase()
                if stop:
                    break

            with contextlib.ExitStack() as st:
                NG = 256
                Wg = sbuf(st, "Wg", [128, 8, 2 * DFF], BF16)
                Wd = sbuf(st, "Wd", [128, NFC, D], BF16)
                B_Wg = [Buf() for _ in range(8)]
                B_Wd = [Buf() for _ in range(NFC)]
                XB = [sbuf(st, "XB%d" % i, [128, 8, NG], BF16) for i in range(2)]
                XF = [sbuf(st, "XF%d" % i, [128, 8, NG], F32) for i in range(2)]
                B_XB = [Buf() for _ in range(2)]
                B_XF = [Buf() for _ in range(2)]
                AT = sbuf(st, "AT", [128, NFC, NG], BF16)
                B_AT = [Buf() for _ in range(NFC)]
                GS = [sbuf(st, "GS%d" % i, [128, NG], F32) for i in range(2)]
                B_GS = [Buf() for _ in range(2)]
                Z2 = sbuf(st, "Z2", [128, 8, NG], F32)
                B_Z2 = Buf()
                XO = [sbuf(st, "XO0", [128, 8, NG], F32)] * 2
                B_XO = [Buf()] * 2
                YO = [sbuf(st, "YO0", [128, D], F32)] * 2
                B_YO = [Buf()] * 2
                pG = [psum(st, "pG%d" % i, [128, 512], F32) for i in range(2)]
                pGu = [psum(st, "pGu%d" % i, [128, 512], F32) for i in range(2)]
                B_pGu = [Buf() for _ in range(2)]
                pD = [psum(st, "pD%d" % i, [128, 512], F32) for i in range(2)]
                pS = psum(st, "pSF", [128, 512], F32)
                pY = [psum(st, "pY%d" % i, [128, 4, 128], F32) for i in range(1)]
                B_pG = [Buf() for _ in range(2)]
                B_pD = [Buf() for _ in range(4)]
                B_pS2 = [Buf(), Buf()]
                B_pY = [Buf()]
                SQ = [sbuf(st, "f_sq%d" % i, [128, NG], F32) for i in range(2)]
                B_SQ = [Buf() for _ in range(2)]
                MEAN = sbuf(st, "f_mean", [128, NG], F32)
                M2 = sbuf(st, "f_m2", [128, NG], F32)
                RSTD = sbuf(st, "f_rstd", [128, NG], F32)
                TT = [sbuf(st, "f_t%d" % i, [128, NG], F32) for i in range(2)]
                B_TT = [Buf() for _ in range(2)]
                B_M, B_M2, B_R = Buf(), Buf(), Buf()

                print("phase F sbuf remaining", nc.sbuf_bytes_remaining)
                wg_src = w_gu_d.ap()[l].rearrange("(c p) n -> p c n", p=128)
                wd_src = w_dn_d.ap()[l].rearrange("(f p) n -> p f n", p=128)
                NWB = (NFC + 3) // 4
                B_Wgb = {(kd, b): Buf() for kd in "gu" for b in range(NWB)}

                def issue_weights():
                    for b in range(NWB):
                        c0, c1 = b * 512, min((b + 1) * 512, DFF)
                        for kd, base in (("g", 0), ("u", DFF)):
                            P.dma("pool", lambda e, c0=c0, c1=c1, base=base: e.dma_start(out=Wg[:, :, base + c0:base + c1],
                                                                                       in_=wg_src[:, :, base + c0:base + c1]),
                                  writes=[B_Wgb[(kd, b)]])
                    for f in range(NFC):
                        P.dma("pool", lambda e, f=f: e.dma_start(out=Wd[:, f, :], in_=wd_src[:, f, :]), writes=[B_Wd[f]])

                groups = [(g * NG, NG) for g in range(NPT * 128 // NG)] + [(NPT * 128, 128)]
                sqi = [0]
                def issue_loads(gi_):
                    t0_, n_ = groups[gi_]
                    i_ = gi_ % 2
                    rd_ = [B_XN[l][t_] for t_ in range(t0_ // 128, (t0_ + n_) // 128)]
                    P.dma("pool", lambda e: e.dma_start(out=XB[i_][:, :, 0:n_], in_=XN[l][:, :, t0_:t0_ + n_]),
                          reads=rd_, writes=[B_XB[i_]])
                    P.dma("sp", lambda e: e.dma_start(out=XF[i_][:, :, 0:n_], in_=XN[l][:, :, t0_:t0_ + n_]),
                          reads=rd_, writes=[B_XF[i_]])

                issue_loads(0)
                issue_weights()
                pendF = []
                for gi, (t0, n) in enumerate(groups):
                    i = gi % 2
                    tl = list(range(t0 // 128, (t0 + n) // 128))
                    for f in range(NFC):
                        if pendF:
                            nf_ = -(-len(pendF) // (NFC - f))
                            for _ in range(nf_):
                                P.replay(pendF.pop(0))
                        pg = pG[f % 2]
                        for c in range(8):
                            P.op("pe", lambda e, pg=pg, f=f, c=c, i=i, n=n: e.matmul(
                                pg[:, 0:n], lhsT=Wg[:, c, f * 128:(f + 1) * 128], rhs=XB[i][:, c, 0:n], start=(c == 0), stop=(c == 7)),
                                reads=[B_XB[i], B_Wgb[("g", f // 4)]], writes=[B_pG[f % 2]])
                        for c in range(8):
                            P.op("pe", lambda e, pg=pg, f=f, c=c, i=i, n=n: e.matmul(
                                pGu[f % 2][:, 0:n], lhsT=Wg[:, c, DFF + f * 128:DFF + (f + 1) * 128], rhs=XB[i][:, c, 0:n],
                                start=(c == 0), stop=(c == 7)), reads=[B_XB[i], B_Wgb[("u", f // 4)]], writes=[B_pGu[f % 2]])
                        P.op("act", lambda e, pg=pg, f=f, n=n: e.activation(out=GS[f % 2][:, 0:n], in_=pg[:, 0:n], func=AF.Silu),
                             reads=[B_pG[f % 2]], writes=[B_GS[f % 2]])
                        P.op("dve", lambda e, f=f, n=n: e.tensor_tensor(out=AT[:, f, 0:n], in0=pGu[f % 2][:, 0:n], in1=GS[f % 2][:, 0:n],
                                                                        op=ALU.mult), reads=[B_pGu[f % 2], B_GS[f % 2]], writes=[B_AT[f]])
                    if gi + 1 < len(groups):
                        issue_loads(gi + 1)
                    for dc in range(8):
                        pd = pD[dc % 2]
                        for f in range(NFC):
                            P.op("pe", lambda e, pd=pd, f=f, dc=dc, n=n: e.matmul(
                                pd[:, 0:n], lhsT=Wd[:, f, dc * 128:(dc + 1) * 128], rhs=AT[:, f, 0:n], start=(f == 0), stop=(f == NFC - 1)),
                                reads=[B_AT[f], B_Wd[f]], writes=[B_pD[dc % 2]])
                        P.op("dve", lambda e, pd=pd, dc=dc, i=i, n=n: e.scalar_tensor_tensor(
                            out=Z2[:, dc, 0:n], in0=XF[i][:, dc, 0:n], scalar=ALPHA, in1=pd[:, 0:n], op0=ALU.mult, op1=ALU.add),
                            reads=[B_XF[i], B_pD[dc % 2]], writes=[B_Z2])
                    P.defer = []
                    s1 = pS[:, 0:n]
                    s2 = pS[:, 256:256 + n]
                    for c in range(8):
                        P.op("pe", lambda e, c=c, n=n, s1=s1: e.matmul(s1, lhsT=ones_f[:], rhs=Z2[:, c, 0:n], start=(c == 0), stop=(c == 7)),
                             reads=[B_Z2, B_const], writes=[B_pS2[0]])
                    for c in range(8):
                        q = sqi[0] % 2
                        sqi[0] += 1
                        P.op("act", lambda e, c=c, q=q, n=n: e.activation(out=SQ[q][:, 0:n], in_=Z2[:, c, 0:n], func=AF.Square),
                             reads=[B_Z2], writes=[B_SQ[q]])
                        P.op("pe", lambda e, c=c, q=q, n=n, s2=s2: e.matmul(s2, lhsT=ones_f[:], rhs=SQ[q][:, 0:n], start=(c == 0), stop=(c == 7)),
                             reads=[B_SQ[q], B_const], writes=[B_pS2[1]])
                    P.op("dve", lambda e, n=n, s1=s1: e.tensor_scalar(out=MEAN[:, 0:n], in0=s1, scalar1=1.0 / D, scalar2=None, op0=ALU.mult),
                         reads=[B_pS2[0], B_pS2[1]], writes=[B_M])
                    P.op("dve", lambda e, n=n: e.tensor_tensor(out=M2[:, 0:n], in0=MEAN[:, 0:n], in1=MEAN[:, 0:n], op=ALU.mult),
                         reads=[B_M], writes=[B_M2])
                    P.op("dve", lambda e, n=n, s2=s2: e.scalar_tensor_tensor(out=M2[:, 0:n], in0=s2, scalar=1.0 / D, in1=M2[:, 0:n],
                                                                             op0=ALU.mult, op1=ALU.subtract),
                         reads=[B_pS2[1], B_M2], writes=[B_M2])
                    P.op("dve", lambda e, n=n: e.tensor_scalar(out=M2[:, 0:n], in0=M2[:, 0:n], scalar1=0.0, scalar2=LN_EPS,
                                                               op0=ALU.max, op1=ALU.add), reads=[B_M2], writes=[B_M2])
                    P.op("act", lambda e, n=n: e.activation(out=RSTD[:, 0:n], in_=M2[:, 0:n], func=AF.Sqrt), reads=[B_M2], writes=[B_R])
                    P.op("dve", lambda e, n=n: e.reciprocal(out=RSTD[:, 0:n], in_=RSTD[:, 0:n]), reads=[B_R], writes=[B_R])
                    for c in range(8):
                        q = c % 2
                        P.op("dve", lambda e, c=c, q=q, n=n: e.tensor_tensor(out=TT[q][:, 0:n], in0=Z2[:, c, 0:n], in1=MEAN[:, 0:n], op=ALU.subtract),
                             reads=[B_Z2, B_M], writes=[B_TT[q]])
                        P.op("dve", lambda e, c=c, q=q, n=n: e.tensor_tensor(out=TT[q][:, 0:n], in0=TT[q][:, 0:n], in1=RSTD[:, 0:n], op=ALU.mult),
                             reads=[B_R, B_TT[q]], writes=[B_TT[q]])
                        P.op("dve", lambda e, c=c, q=q, n=n, i=i: e.tensor_scalar(out=XO[i][:, c, 0:n], in0=TT[q][:, 0:n], scalar1=prm(l, O_L2G + c),
                                                                                   scalar2=prm(l, O_L2B + c), op0=ALU.mult, op1=ALU.add),
                             reads=[B_TT[q], B_const], writes=[B_XO[i]])
                    if l < DEPTH - 1:
                        for t in tl:
                            P.dma("sp", lambda e, i=i, t=t, t0=t0: e.dma_start(out=XT[l + 1][:, :, t * 128:(t + 1) * 128],
                                                                               in_=XO[i][:, :, t * 128 - t0:(t + 1) * 128 - t0]),
                                  reads=[B_XO[i]], writes=[B_XT[l + 1][t]])
                    else:
                        for t in tl:
                            yi = t % 2
                            for hf in range(2):
                                for c in range(4):
                                    P.op("pe", lambda e, c=c, hf=hf, i=i, t=t, t0=t0: e.transpose(
                                        out=pY[0][:, c, :], in_=XO[i][:, hf * 4 + c, t * 128 - t0:(t + 1) * 128 - t0], identity=ident_f[:]),
                                        reads=[B_XO[i], B_const], writes=[B_pY[0]])
                                P.op("act", lambda e, yi=yi, hf=hf: e.activation(out=YO[yi][:, hf * 512:(hf + 1) * 512],
                                                                                 in_=pY[0][:].rearrange("p c k -> p (c k)"), func=AF.Copy),
                                     reads=[B_pY[0]], writes=[B_YO[yi]])
                            P.dma("sp", lambda e, yi=yi, t=t: e.dma_start(out=y_d[t * 128:(t + 1) * 128, :], in_=YO[yi][:]),
                                  reads=[B_YO[yi]], writes=[Buf()])
                    pendF = P.defer
                    P.defer = None
                for it_ in pendF:
                    P.replay(it_)
                pendF = []
                stop = end_phase()
        print("program instructions:", P.n_instr)
    return nc


_NC_CACHE = {}


def _rope_tables():
    half = 32
    inv = (10000.0 ** (-np.arange(half, dtype=np.float32) / half)).astype(np.float32)
    return inv


def kernel(x_prompt, x_sample, cache_k, cache_v, cache_k_idx, state_conv,
           w_in, conv_w, conv_b, conv_ln_g, conv_ln_b, w_o, ln1_g, ln1_b,
           w_gate_up, w_down, ln2_g, ln2_b):
    f32 = np.float32
    x_prompt = np.asarray(x_prompt, f32)
    x_sample = np.asarray(x_sample, f32)
    cache_k = np.asarray(cache_k, f32)
    cache_v = np.asarray(cache_v, f32)
    cache_k_idx = np.asarray(cache_k_idx, f32)
    state_conv = np.asarray(state_conv, f32)
    if "nc" not in _NC_CACHE:
        _NC_CACHE["nc"] = build_program()
    nc = _NC_CACHE["nc"]

    inv = _rope_tables()
    prm = np.zeros((DEPTH, 128, NPRM), f32)
    for l in range(DEPTH):
        cw = np.asarray(conv_w[l], f32).reshape(CONVW, 4, 128)
        prm[l, :, 0:124] = cw.transpose(2, 1, 0).reshape(128, 124)
        prm[l, :, 124:128] = np.asarray(conv_b[l], f32).reshape(4, 128).T
        prm[l, :, 128:132] = np.asarray(conv_ln_g[l], f32).reshape(4, 128).T
        prm[l, :, 132:136] = np.asarray(conv_ln_b[l], f32).reshape(4, 128).T
        prm[l, :, 136:144] = np.asarray(ln1_g[l], f32).reshape(8, 128).T
        prm[l, :, 144:152] = np.asarray(ln1_b[l], f32).reshape(8, 128).T
        prm[l, :, 152:160] = np.asarray(ln2_g[l], f32).reshape(8, 128).T
        prm[l, :, 160:168] = np.asarray(ln2_b[l], f32).reshape(8, 128).T
    w_in = np.ascontiguousarray(np.asarray(w_in, f32))
    w_o = np.ascontiguousarray(np.asarray(w_o, f32))
    w_gu = np.ascontiguousarray(np.asarray(w_gate_up, f32))
    w_dn = np.ascontiguousarray(np.asarray(w_down, f32))

    in_maps = []
    for c in range(8):
        b, r = c // 2, c % 2
        xo = np.empty((TOK, D), f32)
        pos = np.empty((NT, 128), np.int64)
        for j in range(NPT):
            g = 2 * j + r
            xo[j * 128:(j + 1) * 128] = x_prompt[b, g * 128:(g + 1) * 128]
            pos[j] = g * 128 + np.arange(128)
        xo[NPT * 128:NPT * 128 + 64] = x_sample[2 * c]
        xo[NPT * 128 + 64:] = x_sample[2 * c + 1]
        pos[NPT, :64] = SEQ + np.arange(64)
        pos[NPT, 64:] = SEQ + np.arange(64)
        ang = pos.astype(f32)[:, :, None] * inv[None, None, :]
        rope = np.empty((128, NT, 96), f32)
        rope[:, :, 0:32] = np.cos(ang).transpose(1, 0, 2)
        rope[:, :, 32:64] = np.sin(ang).transpose(1, 0, 2)
        rope[:, :, 64:96] = -np.sin(ang).transpose(1, 0, 2)
        qi = np.arange(128)[:, None] // 64
        si = np.arange(128)[None, :] // 64
        causal = np.where(si <= qi, 0.0, NEG).astype(f32)
        meo = np.empty((128, 256), f32)
        if r == 0:
            meo[:, 0:128] = causal
            meo[:, 128:256] = NEG
        else:
            meo[:, 0:128] = 0.0
            meo[:, 128:256] = causal
        flags = np.zeros((128, 2), f32)
        flags[:, r] = 1.0
        in_maps.append({
            "x_own": xo, "rope": rope, "meo": meo, "flags": flags,
            "ck": np.ascontiguousarray(cache_k[:, 2 * c:2 * c + 2].reshape(DEPTH, 2, SEQ, 128)),
            "cv": np.ascontiguousarray(cache_v[:, 2 * c:2 * c + 2].reshape(DEPTH, 2, SEQ, 128)),
            "cki": np.ascontiguousarray(cache_k_idx[:, 2 * c:2 * c + 2]),
            "sconv": np.ascontiguousarray(state_conv[:, 2 * c:2 * c + 2]),
            "w_in": w_in, "prm": prm, "w_o": w_o, "w_gu": w_gu, "w_dn": w_dn,
        })

    res = run_bass_kernel_spmd(nc, in_maps, core_ids=list(range(8)))
    outs = res.results

    y_p = np.empty((4, SEQ, D), f32)
    y_s = np.empty((16, 64, D), f32)
    nk_p = np.empty((DEPTH, 4, SEQ, 2, 64), f32)
    nv_p = np.empty((DEPTH, 4, SEQ, 2, 64), f32)
    nki_p = np.empty((DEPTH, 4, SEQ, 64), f32)
    ncv_p = np.empty((DEPTH, 4, 30, CC), f32)
    nk_s = np.empty((DEPTH, 16, 64, 2, 64), f32)
    nv_s = np.empty((DEPTH, 16, 64, 2, 64), f32)
    nki_s = np.empty((DEPTH, 16, 64, 64), f32)
    ncv_s = np.empty((DEPTH, 16, 30, CC), f32)
    for c in range(8):
        b, r = c // 2, c % 2
        o = outs[c]
        y, nk, nv, nki, ncv = o["y"], o["nk"], o["nv"], o["nki"], o["ncv"]
        for j in range(NPT):
            g = 2 * j + r
            sl = slice(g * 128, (g + 1) * 128)
            y_p[b, sl] = y[j * 128:(j + 1) * 128]
            nk_p[:, b, sl] = nk[:, j * 128:(j + 1) * 128].reshape(DEPTH, 128, 2, 64)
            nv_p[:, b, sl] = nv[:, j * 128:(j + 1) * 128].reshape(DEPTH, 128, 2, 64)
            nki_p[:, b, sl] = nki[:, j * 128:(j + 1) * 128]
        if r == 1:
            ncv_p[:, b] = ncv[:, 0]
        for s_ in range(2):
            q = 2 * c + s_
            rows = slice(NPT * 128 + s_ * 64, NPT * 128 + (s_ + 1) * 64)
            y_s[q] = y[rows]
            nk_s[:, q] = nk[:, rows].reshape(DEPTH, 64, 2, 64)
            nv_s[:, q] = nv[:, rows].reshape(DEPTH, 64, 2, 64)
            nki_s[:, q] = nki[:, rows]
            ncv_s[:, q] = ncv[:, 1 + s_]
    return (y_p, y_s, nk_p, nv_p, nki_p, ncv_p, nk_s, nv_s, nki_s, ncv_s)
```

```python
import contextlib
import numpy as np
import concourse.bass as bass
import concourse.mybir as mybir
from concourse.bass_utils import run_bass_kernel_spmd

F32 = mybir.dt.float32
BF16 = mybir.dt.bfloat16
AF = mybir.ActivationFunctionType
ALU = mybir.AluOpType
AX = mybir.AxisListType

D = 1024
DEPTH = 2
SEQ = 4096
NPT = 16
NT = 17
TOK = NT * 128
CONVW = 31
CC = 512
DFF = 2816
NFC = 22
ALPHA = float((2 * DEPTH) ** 0.25)
ATTN_SCALE = 64 ** -0.5
WI_SCALE = float((64 ** -0.5) * (8 ** -0.5))
LN_EPS = 1e-5
TOPK = 256
NBIS = 20
NEG = -1.0e30
NPRM = 168
EXW = 10240

ENGS = ("pe", "act", "dve", "pool", "sp")
SEM_EPOCH = 30000
N_EPOCH = 3
DMA_POOL = 20
RELAX_SAME_ENGINE = False
PE_ACC_NOWAIT = True

import os
LAST_PHASE = int(os.environ.get("K_LAST_PHASE", "99"))
NO_CC = False
A_MODE = int(os.environ.get("K_A_MODE", "255"))
F_MODE = int(os.environ.get("K_F_MODE", "255"))
LN_STEPS = 99


class Buf:
    __slots__ = ("name", "w", "r")

    def __init__(self, name=""):
        self.name = name
        self.w = None
        self.r = {}


class _PeProbe:
    start = True

    def matmul(self, *a, **kw):
        self.start = kw.get("start", True)
        return self

    def transpose(self, *a, **kw):
        self.start = True
        return self


class Prog:
    def __init__(self, nc, st):
        self.nc = nc
        self.ops = {e: [] for e in ENGS}
        self.cnt = {e: 0 for e in ENGS}
        self.waited = {e: {} for e in ENGS}
        self.dma_rr = {e: 0 for e in ENGS}
        self.dma_val = {}
        self.sems = {}
        i = 0
        for e in ENGS:
            for ep in range(N_EPOCH):
                self.sems[("E", e, ep)] = st.enter_context(nc.semaphore("s%d" % i)); i += 1
        for e in ("sp", "pool", "act"):
            for j in range(DMA_POOL):
                self.sems[("D", e, j)] = st.enter_context(nc.semaphore("s%d" % i)); i += 1
        self.n_instr = 0
        self.dummy = None
        self.last_raw = {}
        self.defer = None

    def replay(self, item):
        kind, eng, fn, reads, writes, inc = item
        if kind == "op":
            self.op(eng, fn, reads, writes)
        else:
            self.dma(eng, fn, reads, writes, inc)

    def _wait(self, eng, ev):
        if ev is None:
            return
        k, v = ev
        if self.waited[eng].get(k, 0) >= v:
            return
        if getattr(self, "pe_cont", False) and k[0] == "E" and k[1] == "pe" and eng == "pe":
            return
        if RELAX_SAME_ENGINE and k[0] == "E" and k[1] == eng:
            cur_ep = self.cnt[eng] // SEM_EPOCH
            if eng in ("act", "dve"):
                prod = k[2] * SEM_EPOCH + v
                if self.cnt[eng] - prod >= 1:
                    return
                if eng == "dve" and self.dummy is not None and self.last_raw.get(eng) != self.cnt[eng]:
                    d = self.dummy
                    self.ops[eng].append(("raw", lambda e: e.memset(d, 0.0)))
                    self.last_raw[eng] = self.cnt[eng]
                    return
                if eng == "dve" and self.dummy is not None:
                    return
        self.waited[eng][k] = v
        self.ops[eng].append(("wait", k, v))

    def _deps(self, eng, reads, writes):
        for b in reads:
            self._wait(eng, b.w)
        for b in writes:
            self._wait(eng, b.w)
            for ev in b.r.items():
                self._wait(eng, ev)

    def _commit(self, ev, reads, writes):
        for b in writes:
            b.w = ev
            b.r = {}
        for b in reads:
            if b in writes:
                continue
            if b.r.get(ev[0], 0) < ev[1]:
                b.r[ev[0]] = ev[1]

    def op(self, eng, fn, reads=(), writes=()):
        if self.defer is not None:
            self.defer.append(("op", eng, fn, tuple(reads), tuple(writes), 0))
            return
        self.pe_cont = False
        if eng == "pe" and PE_ACC_NOWAIT:
            pr = _PeProbe()
            fn(pr)
            self.pe_cont = (pr.start is False)
        self._deps(eng, reads, writes)
        self.pe_cont = False
        self.cnt[eng] += 1
        c = self.cnt[eng]
        k = ("E", eng, (c - 1) // SEM_EPOCH)
        v = (c - 1) % SEM_EPOCH + 1
        self.ops[eng].append(("op", fn, k, 1))
        self._commit((k, v), reads, writes)
        self.n_instr += 1

    def dma(self, eng, fn, reads=(), writes=(), inc=16):
        if self.defer is not None:
            self.defer.append(("dma", eng, fn, tuple(reads), tuple(writes), inc))
            return
        i = self.dma_rr[eng] % DMA_POOL
        self.dma_rr[eng] += 1
        k = ("D", eng, i)
        prev = self.dma_val.get(k, 0)
        if prev:
            self._wait(eng, (k, prev))
        self._deps(eng, reads, writes)
        v = prev + inc
        self.dma_val[k] = v
        self.ops[eng].append(("op", fn, k, inc))
        self._commit((k, v), reads, writes)
        self.n_instr += 1

    def barrier(self):
        evs = []
        for e in ENGS:
            c = self.cnt[e]
            if c:
                evs.append((("E", e, (c - 1) // SEM_EPOCH), (c - 1) % SEM_EPOCH + 1))
        for k, v in self.dma_val.items():
            evs.append((k, v))
        for e in ENGS:
            for ev in evs:
                self._wait(e, ev)

    def emit(self):
        nc = self.nc
        ops = self.ops
        sems = self.sems

        def run(e, lst):
            for it in lst:
                if it[0] == "wait":
                    e.wait_ge(sems[it[1]], it[2])
                elif it[0] == "raw":
                    it[1](e)
                else:
                    it[1](e).then_inc(sems[it[2]], it[3])

        with nc.Block() as block:
            @block.tensor
            def _(e):
                run(e, ops["pe"])

            @block.scalar
            def _(e):
                run(e, ops["act"])

            @block.vector
            def _(e):
                run(e, ops["dve"])

            @block.gpsimd
            def _(e):
                run(e, ops["pool"])

            @block.sync
            def _(e):
                run(e, ops["sp"])
        self.ops = {e: [] for e in ENGS}


class Ring:
    def __init__(self, items):
        self.items = items
        self.i = 0

    def next(self):
        it = self.items[self.i % len(self.items)]
        self.i += 1
        return it


def build_program():
    nc = bass.Bass("TRN2", target_bir_lowering=False)

    def din(name, shape, dt=F32):
        return nc.dram_tensor(name, shape, dt, kind="ExternalInput")

    def dout(name, shape, dt=F32):
        return nc.dram_tensor(name, shape, dt, kind="ExternalOutput")

    x_own = din("x_own", [TOK, D])
    rope_d = din("rope", [128, NT, 96])
    meo_d = din("meo", [128, 256])
    flags_d = din("flags", [128, 2])
    ck_d = din("ck", [DEPTH, 2, SEQ, 128])
    cv_d = din("cv", [DEPTH, 2, SEQ, 128])
    cki_d = din("cki", [DEPTH, 2, SEQ, 64])
    sconv_d = din("sconv", [DEPTH, 2, 30, CC])
    w_in_d = din("w_in", [DEPTH, D, 2376])
    prm_d = din("prm", [DEPTH, 128, NPRM])
    w_o_d = din("w_o", [DEPTH, D, D])
    w_gu_d = din("w_gu", [DEPTH, D, 2 * DFF])
    w_dn_d = din("w_dn", [DEPTH, DFF, D])

    y_d = dout("y", [TOK, D])
    nk_d = dout("nk", [DEPTH, TOK, 128])
    nv_d = dout("nv", [DEPTH, TOK, 128])
    nki_d = dout("nki", [DEPTH, TOK, 64])
    ncv_d = dout("ncv", [DEPTH, 3, 30, CC])

    XT = [nc.dram_tensor("xt%d" % l, [128, 8, TOK], F32) for l in range(DEPTH)]
    XN = [nc.dram_tensor("xn%d" % l, [128, 8, TOK], F32) for l in range(DEPTH)]
    EXP = ((0, 4096), (4096, 4096), (8192, 2048))
    IB = [[nc.dram_tensor("ib%d_%d" % (l, k), [128, w_], BF16) for k, (c_, w_) in enumerate(EXP)] for l in range(DEPTH)]
    OB = [[nc.dram_tensor("ob%d_%d" % (l, k), [256, w_], BF16) for k, (c_, w_) in enumerate(EXP)] for l in range(DEPTH)]

    def ob_ap(l, r, col, n):
        for k, (c_, w_) in enumerate(EXP):
            if c_ <= col and col + n <= c_ + w_:
                return OB[l][k][r * 128:(r + 1) * 128, col - c_:col - c_ + n]
        raise AssertionError
    B_XT = [[Buf() for _ in range(NT)] for _ in range(DEPTH)]
    B_XN = [[Buf() for _ in range(NT)] for _ in range(DEPTH)]
    B_IB = [Buf() for _ in range(DEPTH)]
    B_OB = [Buf() for _ in range(DEPTH)]

    phase_no = [0]

    with contextlib.ExitStack() as gst:
        P = Prog(nc, gst)

        uid = [0]

        def sbuf(st, name, shape, dt):
            uid[0] += 1
            return st.enter_context(nc.sbuf_tensor("%s_%d" % (name, uid[0]), shape, dt))

        def psum(st, name, shape, dt):
            uid[0] += 1
            return st.enter_context(nc.psum_tensor("%s_%d" % (name, uid[0]), shape, dt))

        ident_f = sbuf(gst, "ident_f", [128, 128], F32)
        ident_b = sbuf(gst, "ident_b", [128, 128], BF16)
        ones_f = sbuf(gst, "ones_f", [128, 128], F32)
        sel_f = sbuf(gst, "sel_f", [65, 64], F32)
        meo_s = sbuf(gst, "meo_s", [128, 256], F32)
        flags_s = sbuf(gst, "flags_s", [128, 2], F32)
        prm_s = sbuf(gst, "prm_s", [128, DEPTH, NPRM], F32)
        B_const = Buf("const")
        CK = sbuf(gst, "CK", [128, NBIS + 1], F32)
        dummy_t = sbuf(gst, "dummy_t", [128, 8], F32)
        P.dummy = None

        P.op("pool", lambda e: e.memset(ident_f[:], 0.0), writes=[B_const])
        P.op("pool", lambda e: e.affine_select(out=ident_f[:], in_=ident_f[:], pattern=[[-1, 128]],
                                               compare_op=ALU.not_equal, fill=1.0, base=0,
                                               channel_multiplier=1), writes=[B_const])
        P.op("pool", lambda e: e.tensor_copy(out=ident_b[:], in_=ident_f[:]), writes=[B_const])
        P.op("pool", lambda e: e.memset(ones_f[:], 1.0), writes=[B_const])
        for k_ in range(NBIS + 1):
            P.op("pool", lambda e, k_=k_: e.memset(CK[:, k_:k_ + 1], float(2.0 ** -k_)), writes=[B_const])
        P.op("pool", lambda e: e.memset(sel_f[:], 0.0), writes=[B_const])
        P.op("pool", lambda e: e.memset(sel_f[64:65, :], 1.0), writes=[B_const])
        P.dma("sp", lambda e: e.dma_start(out=meo_s[:], in_=meo_d[:, :]), writes=[B_const])
        P.dma("sp", lambda e: e.dma_start(out=flags_s[:], in_=flags_d[:, :]), writes=[B_const])
        for l in range(DEPTH):
            P.dma("sp", lambda e, l=l: e.dma_start(out=prm_s[:, l, :], in_=prm_d[l, :, :]), writes=[B_const])

        def prm(l, off, n=1):
            return prm_s[:, l, off:off + n]
        O_CW, O_CB, O_CG, O_CBB, O_L1G, O_L1B, O_L2G, O_L2B = 0, 124, 128, 132, 136, 144, 152, 160

        def end_phase():
            P.barrier()
            P.emit()
            phase_no[0] += 1
            return phase_no[0] >= LAST_PHASE

        with contextlib.ExitStack() as st:
            xin = [sbuf(st, "xin%d" % i, [128, D], F32) for i in range(2)]
            xo = [sbuf(st, "xo%d" % i, [128, 8, 128], F32) for i in range(2)]
            pX = [psum(st, "pX%d" % i, [128, 8, 128], F32) for i in range(2)]
            B_xin = [Buf() for _ in range(2)]
            B_xo = [Buf() for _ in range(2)]
            B_pX = [Buf() for _ in range(2)]
            B_pXb = [[Buf(), Buf()] for _ in range(2)]
            for t in range(NT):
                i = t % 2
                P.dma("sp", lambda e, t=t, i=i: e.dma_start(out=xin[i][:], in_=x_own[t * 128:(t + 1) * 128, :]),
                      writes=[B_xin[i]])
                for c in (0, 4, 1, 5, 2, 6, 3, 7):
                    P.op("pe", lambda e, c=c, i=i: e.transpose(out=pX[i][:, c, :], in_=xin[i][:, c * 128:(c + 1) * 128],
                                                               identity=ident_f[:]),
                         reads=[B_xin[i], B_const], writes=[B_pXb[i][c // 4]])
                P.op("act", lambda e, i=i: e.activation(out=xo[i][:], in_=pX[i][:], func=AF.Copy),
                     reads=B_pXb[i], writes=[B_xo[i]])
                P.dma("sp", lambda e, t=t, i=i: e.dma_start(out=XT[0][:, :, t * 128:(t + 1) * 128], in_=xo[i][:]),
                      reads=[B_xo[i]], writes=[B_XT[0][t]])
            stop = end_phase()

        for l in range(DEPTH):
            if stop:
                break
            with contextlib.ExitStack() as pa:
                QT = sbuf(pa, "QT", [128, NT, 512], BF16)
                QIT = sbuf(pa, "QIT", [128, NT, 512], BF16)
                UT = sbuf(pa, "UT", [128, 4, NPT * 160 + 2 * 96], BF16)
                WI = sbuf(pa, "WI", [128, NT + 1, 8], F32)
                SNK = sbuf(pa, "SNK", [128, 3, 128], BF16)
                SNV = sbuf(pa, "SNV", [64, 2, 128], BF16)
                B_QT = [Buf() for _ in range(NT)]
                B_QIT = [Buf() for _ in range(NT)]
                B_UT = Buf()
                B_UTh = Buf()
                B_WI = Buf()
                B_SN = Buf()

                def useg(t):
                    if t < NPT:
                        return [(t * 160, 128, 0)]
                    return [(NPT * 160, 64, 0), (NPT * 160 + 96, 64, 64)]

                with contextlib.ExitStack() as st:
                    W = sbuf(st, "Win", [128, 8, 2376], BF16)
                    B_W = Buf()
                    rope_s = sbuf(st, "rope_s", [128, NT, 96], F32)
                    B_rope = Buf()
                    P.dma("sp", lambda e: e.dma_start(out=rope_s[:], in_=rope_d[:, :, :]), writes=[B_rope])
                    EXS = sbuf(st, "EXS", [128, EXW], BF16)
                    B_EXS = Buf()
                    xTb = [sbuf(st, "xTb%d" % i, [128, 8, 128], BF16) for i in range(3)]
                    B_xTb = [Buf() for _ in range(3)]
                    PT = [sbuf(st, "PT%d" % i, [128, 1352], F32) for i in range(2)]
                    B_PT = [Buf() for _ in range(2)]
                    T1 = sbuf(st, "T1", [128, 1216], F32)
                    T2 = sbuf(st, "T2", [128, 1216], F32)
                    RT = [sbuf(st, "RT%d" % i, [128, 1216], F32) for i in range(2)]
                    B_T1, B_T2 = Buf(), Buf()
                    B_RT = [Buf() for _ in range(2)]
                    TB = [sbuf(st, "TB%d" % i, [128, 1408], BF16) for i in range(2)]
                    B_TB = [Buf() for _ in range(2)]
                    SG = sbuf(st, "SG", [128, 4, 128], F32)
                    U = [sbuf(st, "U%d" % i, [128, 4, 128], F32) for i in range(2)]
                    B_SG = Buf()
                    B_U = [Buf() for _ in range(2)]
                    NCS = sbuf(st, "NCS", [128, 512], F32)
                    B_NCS = Buf()
                    PV2 = sbuf(st, "PV2", [64, 136], F32)
                    B_PV2 = Buf()
                    pA = psum(st, "pA", [128, 512], F32)
                    pB = psum(st, "pB", [128, 512], F32)
                    pC = psum(st, "pC", [128, 512], F32)
                    pAG = psum(st, "pAG", [128, 8, 128], F32)
                    pTr = psum(st, "pTr", [128, 2048], BF16)
                    pV2 = psum(st, "pV2", [128, 512], F32)
                    B_pA, B_pB, B_pC, B_pAG, B_pTr, B_pV2 = Buf(), Buf(), Buf(), Buf(), Buf(), Buf()
                    B_pAGb = [Buf(), Buf()]
                    B_pTrb = [Buf(), Buf()]

                    print("phase P sbuf remaining", nc.sbuf_bytes_remaining)
                    def load_x(t_):
                        P.dma("pool", lambda e: e.dma_start(out=xTb[t_ % 3][:], in_=XT[l][:, :, t_ * 128:(t_ + 1) * 128]),
                              reads=[B_XT[l][t_]], writes=[B_xTb[t_ % 3]])

                    load_x(0)
                    load_x(1)
                    wsrc = w_in_d.ap()[l].rearrange("(c p) n -> p c n", p=128)
                    for (s0, s1, d0) in ((0, 1536, 0), (1792, 2304, 1536), (1536, 1664, 2048), (2304, 2368, 2176),
                                         (1664, 1792, 2240), (2368, 2376, 2368)):
                        P.dma("pool", lambda e, s0=s0, s1=s1, d0=d0: e.dma_start(out=W[:, :, d0:d0 + (s1 - s0)],
                                                                                  in_=wsrc[:, :, s0:s1]), writes=[B_W])
                    P.op("pool", lambda e: e.memset(EXS[:], 0.0), writes=[B_EXS])
                    P.op("pool", lambda e: e.memset(UT[:], 0.0), writes=[B_UT, B_UTh])

                    def p_stage1(t):
                        i = t % 2
                        if t + 2 < NT:
                            load_x(t + 2)
                        for (pp, Bp, c0, nn) in ((pA, B_pA, 1024, 512), (pB, B_pB, 1536, 512), (pC, B_pC, 2048, 328)):
                            for c in range(8):
                                P.op("pe", lambda e, pp=pp, c=c, c0=c0, nn=nn, i=i: e.matmul(
                                    pp[:, 0:nn], lhsT=xTb[t % 3][:, c, :], rhs=W[:, c, c0:c0 + nn], start=(c == 0), stop=(c == 7)),
                                    reads=[B_xTb[t % 3], B_W], writes=[Bp])
                        P.op("act", lambda e, i=i: e.activation(out=PT[i][:, 0:512], in_=pA[:, 0:512], func=AF.Copy),
                             reads=[B_pA], writes=[B_PT[i]])
                        P.op("act", lambda e, i=i: e.activation(out=PT[i][:, 512:1024], in_=pB[:, 0:512], func=AF.Copy),
                             reads=[B_pB], writes=[B_PT[i]])
                        P.op("act", lambda e, i=i: e.activation(out=PT[i][:, 1024:1352], in_=pC[:, 0:328], func=AF.Copy),
                             reads=[B_pC], writes=[B_PT[i]])
                        for f in (0, 4, 1, 5, 2, 6, 3, 7):
                            for c in range(8):
                                P.op("pe", lambda e, f=f, c=c, i=i: e.matmul(
                                    pAG[:, f, :], lhsT=W[:, c, f * 128:(f + 1) * 128], rhs=xTb[t % 3][:, c, :],
                                    start=(c == 0), stop=(c == 7)), reads=[B_xTb[t % 3], B_W], writes=[B_pAGb[f // 4]])
                        if t == NT - 1:
                            for c in range(8):
                                P.op("pe", lambda e, c=c, i=i: e.matmul(
                                    pV2[0:64, 0:136], lhsT=xTb[t % 3][:, c, 64:128], rhs=W[:, c, 2240:2376],
                                    start=(c == 0), stop=(c == 7)), reads=[B_xTb[t % 3], B_W], writes=[B_pV2])
                            P.op("act", lambda e: e.activation(out=PV2[:], in_=pV2[0:64, 0:136], func=AF.Copy),
                                 reads=[B_pV2], writes=[B_PV2])

                    def p_stage1b(t):
                        i = t % 2
                        P.op("act", lambda e: e.activation(out=SG[:], in_=pAG[:, 4:8, :], func=AF.Sigmoid),
                             reads=[B_pAGb[1]], writes=[B_SG])
                        P.op("dve", lambda e, i=i: e.tensor_tensor(out=U[i][:], in0=pAG[:, 0:4, :], in1=SG[:], op=ALU.mult),
                             reads=[B_pAGb[0], B_SG], writes=[B_U[i]])

                    def p_stage2(t):
                        i = t % 2
                        Hv = PT[i][:, 0:1216].rearrange("p (h t d) -> p h t d", t=2, d=32)
                        T1v = T1[:].rearrange("p (h t d) -> p h t d", t=2, d=32)
                        T2v = T2[:].rearrange("p (h t d) -> p h t d", t=2, d=32)
                        cosb = rope_s[:, t, 0:32].unsqueeze(1).unsqueeze(1).broadcast_to([128, 19, 2, 32])
                        sinb = rope_s[:, t, 32:64].unsqueeze(1).broadcast_to([128, 19, 32])
                        nsinb = rope_s[:, t, 64:96].unsqueeze(1).broadcast_to([128, 19, 32])
                        P.op("dve", lambda e, Hv=Hv, T1v=T1v, cosb=cosb: e.tensor_tensor(out=T1v, in0=Hv, in1=cosb, op=ALU.mult),
                             reads=[B_PT[i], B_rope], writes=[B_T1])
                        P.op("dve", lambda e, Hv=Hv, T2v=T2v, nsinb=nsinb: e.tensor_tensor(
                            out=T2v[:, :, 0, :], in0=Hv[:, :, 1, :], in1=nsinb, op=ALU.mult),
                            reads=[B_PT[i], B_rope], writes=[B_T2])
                        P.op("dve", lambda e, Hv=Hv, T2v=T2v, sinb=sinb: e.tensor_tensor(
                            out=T2v[:, :, 1, :], in0=Hv[:, :, 0, :], in1=sinb, op=ALU.mult),
                            reads=[B_PT[i], B_rope], writes=[B_T2])
                        P.op("dve", lambda e, i=i: e.tensor_tensor(out=RT[i][:], in0=T1[:], in1=T2[:], op=ALU.add),
                             reads=[B_T1, B_T2], writes=[B_RT[i]])
                        r0, r1 = t * 128, (t + 1) * 128
                        P.dma("sp", lambda e, i=i, r0=r0, r1=r1: e.dma_start(out=nk_d[l, r0:r1, :], in_=RT[i][:, 1024:1152]),
                              reads=[B_RT[i]], writes=[Buf()])
                        P.dma("sp", lambda e, i=i, r0=r0, r1=r1: e.dma_start(out=nki_d[l, r0:r1, :], in_=RT[i][:, 1152:1216]),
                              reads=[B_RT[i]], writes=[Buf()])
                        P.dma("sp", lambda e, i=i, r0=r0, r1=r1: e.dma_start(out=nv_d[l, r0:r1, :], in_=PT[i][:, 1216:1344]),
                              reads=[B_PT[i]], writes=[Buf()])
                        P.op("dve", lambda e, t=t, i=i: e.tensor_scalar(out=WI[:, t, :], in0=PT[i][:, 1344:1352], scalar1=WI_SCALE,
                                                                         scalar2=None, op0=ALU.mult), reads=[B_PT[i]], writes=[B_WI])
                        if t == NT - 1:
                            P.op("dve", lambda e: e.tensor_scalar(out=WI[0:64, NT, :], in0=PV2[:, 128:136], scalar1=WI_SCALE,
                                                                   scalar2=None, op0=ALU.mult), reads=[B_PV2], writes=[B_WI])
                            P.op("dve", lambda e, i=i: e.tensor_copy(out=SNV[:, 0, :], in_=PT[i][0:64, 1216:1344]),
                                 reads=[B_PT[i]], writes=[B_SN])
                            P.op("dve", lambda e: e.tensor_copy(out=SNV[:, 1, :], in_=PV2[:, 0:128]),
                                 reads=[B_PV2], writes=[B_SN])
                        P.op("act", lambda e, i=i: e.activation(out=TB[i][:, 0:1024], in_=RT[i][:, 0:1024], func=AF.Copy),
                             reads=[B_RT[i]], writes=[B_TB[i]])
                        kd_o = TB[i][:, 1024:1280].rearrange("p (g r d) -> p g r d", g=2, r=2)
                        kd_i = RT[i][:, 1024:1152].rearrange("p (g d) -> p g d", g=2).unsqueeze(2).broadcast_to([128, 2, 2, 64])
                        P.op("dve", lambda e, kd_o=kd_o, kd_i=kd_i: e.tensor_copy(out=kd_o, in_=kd_i),
                             reads=[B_RT[i]], writes=[B_TB[i]])
                        ki_o = TB[i][:, 1280:1408].rearrange("p (r d) -> p r d", r=2)
                        ki_i = RT[i][:, 1152:1216].unsqueeze(1).broadcast_to([128, 2, 64])
                        P.op("dve", lambda e, ki_o=ki_o, ki_i=ki_i: e.tensor_copy(out=ki_o, in_=ki_i),
                             reads=[B_RT[i]], writes=[B_TB[i]])
                        for c in (0, 8, 1, 9, 2, 10, 3, 4, 5, 6, 7):
                            P.op("pe", lambda e, c=c, i=i: e.transpose(out=pTr[:, c * 128:(c + 1) * 128],
                                                                       in_=TB[i][:, c * 128:(c + 1) * 128], identity=ident_b[:]),
                                 reads=[B_TB[i], B_const], writes=[B_pTrb[c // 8]])
                        P.op("act", lambda e, t=t: e.activation(out=QT[:, t, :], in_=pTr[:, 0:512], func=AF.Copy),
                             reads=[B_pTrb[0]], writes=[B_QT[t]])
                        P.op("act", lambda e, t=t: e.activation(out=QIT[:, t, :], in_=pTr[:, 512:1024], func=AF.Copy),
                             reads=[B_pTrb[0]], writes=[B_QIT[t]])
                        if t < NPT:
                            exo = EXS[:, 0:6144].rearrange("p (a t k) -> p a t k", a=3, k=128)[:, :, t, :]
                            P.op("act", lambda e, exo=exo: e.activation(out=exo, in_=pTr[:, 1024:1408].rearrange("p (a k) -> p a k", a=3),
                                                                        func=AF.Copy), reads=[B_pTrb[1]], writes=[B_EXS])
                            P.op("dve", lambda e, t=t, i=i: e.tensor_copy(out=EXS[:, 6144 + t * 128:6144 + (t + 1) * 128],
                                                                           in_=PT[i][:, 1216:1344]), reads=[B_PT[i]], writes=[B_EXS])
                        else:
                            P.op("act", lambda e: e.activation(out=SNK[:], in_=pTr[:, 1024:1408].rearrange("p (a k) -> p a k", a=3),
                                                               func=AF.Copy), reads=[B_pTrb[1]], writes=[B_SN])
                        for (off, n, t0) in useg(t):
                            P.op("dve", lambda e, off=off, n=n, t0=t0, i=i: e.tensor_copy(
                                out=UT[:, :, off + 32:off + 32 + n], in_=U[i][:, :, t0:t0 + n]), reads=[B_U[i]], writes=[B_UT])
                        if t < NPT:
                            tl = EXS[:, 8192:10240].rearrange("p (c t k) -> p c t k", c=4, k=32)[:, :, t, :]
                            P.op("dve", lambda e, tl=tl, i=i: e.tensor_copy(out=tl, in_=U[i][:, :, 96:128]),
                                 reads=[B_U[i]], writes=[B_EXS])
                        if t >= NPT - 1:
                            for c in range(4):
                                P.op("pe", lambda e, c=c, i=i: e.transpose(out=pA[:, c * 128:(c + 1) * 128], in_=U[i][:, c, :],
                                                                           identity=ident_f[:]),
                                     reads=[B_U[i], B_const], writes=[B_pA])
                            P.op("act", lambda e: e.activation(out=NCS[:], in_=pA[:, 0:512], func=AF.Copy),
                                 reads=[B_pA], writes=[B_NCS])
                            if t == NPT - 1:
                                P.dma("sp", lambda e: e.dma_start(out=ncv_d[l, 0, :, :], in_=NCS[98:128, :]),
                                      reads=[B_NCS], writes=[Buf()])
                            else:
                                P.dma("sp", lambda e: e.dma_start(out=ncv_d[l, 1, :, :], in_=NCS[34:64, :]),
                                      reads=[B_NCS], writes=[Buf()])
                                P.dma("sp", lambda e: e.dma_start(out=ncv_d[l, 2, :, :], in_=NCS[98:128, :]),
                                      reads=[B_NCS], writes=[Buf()])

                    for t in range(NT + 1):
                        if t < NT:
                            p_stage1(t)
                        if t >= 1:
                            p_stage2(t - 1)
                        if t < NT:
                            p_stage1b(t)
                    for k, (c_, w_) in enumerate(EXP):
                        P.dma("sp", lambda e, k=k, c_=c_, w_=w_: e.dma_start(out=IB[l][k][:, :], in_=EXS[:, c_:c_ + w_]),
                              reads=[B_EXS], writes=[B_IB[l]])
                    if not NO_CC:
                        for k in range(len(EXP)):
                            P.dma("pool", lambda e, k=k: e.collective_compute(
                                "AllGather", ALU.bypass, replica_groups=[[0, 1], [2, 3], [4, 5], [6, 7]],
                                ins=[IB[l][k].ap().opt()], outs=[OB[l][k].ap().opt()]),
                                reads=[B_IB[l]], writes=[B_OB[l]], inc=1)
                    stop = end_phase()
                if stop:
                    break

                with contextlib.ExitStack() as st:
                    NS = 33
                    K0 = sbuf(st, "K0", [128, NS * 128], BF16)
                    K1 = sbuf(st, "K1", [128, NS * 128], BF16)
                    KI = sbuf(st, "KI", [128, NS * 128], BF16)
                    VX = sbuf(st, "VX", [128, NS, 2, 65], BF16)
                    B_K = Buf()
                    B_V = Buf()
                    CKS = [sbuf(st, "CKS%d" % i, [128, 8, 128], BF16) for i in range(2)]
                    CKD = [sbuf(st, "CKD0", [128, 8, 384], BF16)] * 2
                    B_CKS = [Buf() for _ in range(2)]
                    B_CKD = [Buf()] * 2
                    CIS = [sbuf(st, "CIS%d" % i, [128, 8, 64], BF16) for i in range(2)]
                    B_CIS = [Buf() for _ in range(2)]
                    VST = [sbuf(st, "VST%d" % i, [128, 8, 128], BF16) for i in range(2)]
                    B_VST = [Buf() for _ in range(2)]
                    SCV = sbuf(st, "SCV", [32, 512], F32)
                    B_SCV = Buf()
                    SC = sbuf(st, "SC", [128, NS * 128], F32)
                    B_SC = Buf()
                    RTL = SC[:, 0:2048].bitcast(BF16).rearrange("p (r k) -> p r k", r=2)
                    B_RTL = B_SC
                    M01 = sbuf(st, "M01", [128, NS * 128], BF16)
                    B_M01 = Buf()
                    MT = [sbuf(st, "MT%d" % i, [128, 8, 128], BF16) for i in range(2)]
                    B_MT = [Buf() for _ in range(2)]
                    RB = [sbuf(st, "RB%d" % i, [128, 512], F32) for i in range(3)]
                    B_RB = [Buf() for _ in range(3)]
                    EBA = [sbuf(st, "EBA%d" % i, [128, 1024], BF16) for i in range(3)]
                    B_EBA = [Buf() for _ in range(3)]
                    PBA = [sbuf(st, "PBA%d" % i, [128, 1024], BF16) for i in range(3)]
                    B_PBA = [Buf() for _ in range(3)]
                    OS = sbuf(st, "OS", [65, 512], F32)
                    B_OS, B_RD = Buf(), Buf()
                    BS = sbuf(st, "BS", [128, 40], F32)
                    B_BS = Buf()
                    WoC = sbuf(st, "WoC", [128, 4, D], BF16)
                    WoA = sbuf(st, "WoA", [64, 8, D], BF16)
                    B_Wo = Buf()
                    ATT = sbuf(st, "ATT", [64, 8, 128], BF16)
                    B_ATT = Buf()
                    CAC = sbuf(st, "CAC", [128, 4, 128], F32)
                    B_CAC = Buf()
                    B_CACc = [Buf() for _ in range(4)]
                    CVT = sbuf(st, "CVT", [128, 4, 128], BF16)
                    B_CVT = Buf()
                    Z = sbuf(st, "Z", [128, 8, 128], F32)
                    B_Z = Buf()
                    XFc = [sbuf(st, "XFc%d" % i, [128, 128], F32) for i in range(2)]
                    B_XFc = [Buf() for _ in range(2)]
                    XNo = [sbuf(st, "XNo0", [128, 8, 128], F32)] * 2
                    B_XNo = [Buf()] * 2
                    pI = [psum(st, "pI%d" % i, [128, 512], F32) for i in range(2)]
                    pL = [psum(st, "pL%d" % i, [128, 512], F32) for i in range(2)]
                    pO = [psum(st, "pO%d" % i, [128, 512], F32) for i in range(2)]
                    pM = psum(st, "pM", [128, 1024], BF16)
                    pS = psum(st, "pS", [128, 512], F32)
                    pSb = pS[:].bitcast(BF16)
                    B_pI = [Buf() for _ in range(2)]
                    B_pL = [Buf() for _ in range(2)]
                    B_pO = [Buf() for _ in range(2)]
                    B_pM, B_pS = Buf(), Buf()
                    B_pS2 = [Buf(), Buf()]

                    wo_c = w_o_d.ap()[l][0:512, :].rearrange("(c p) n -> p c n", p=128)
                    wo_a = w_o_d.ap()[l][512:1024, :].rearrange("(h p) n -> p h n", p=64)
                    P.dma("pool", lambda e: e.dma_start(out=WoC[:], in_=wo_c), writes=[B_Wo])
                    P.dma("pool", lambda e: e.dma_start(out=WoA[:], in_=wo_a), writes=[B_Wo])
                    P.op("pool", lambda e: e.memset(VX[:], 1.0), writes=[B_V])


                    rbi = [0]
                    ebi = [0]

                    def attend(t, tok0, nq, wi_ap, ntiles, last_keys, masked, att_tok0, filler=None, bis_filler=None, part="all"):
                        S = (ntiles - 1) * 128 + last_keys
                        if part in ("all", "pre"):
                            nblk = (S + 511) // 512
                            for b in range(nblk):
                                c0 = b * 512
                                n = min(512, S - c0)
                                for h in range(8):
                                    pi = pI[h % 2]
                                    half = (h % 2) * 64
                                    ch = h // 2
                                    P.op("pe", lambda e, pi=pi, half=half, ch=ch, c0=c0, n=n: e.matmul(
                                        pi[0:nq, 0:n], lhsT=QIT[half:half + 64, t, ch * 128 + tok0:ch * 128 + tok0 + nq],
                                        rhs=KI[half:half + 64, c0:c0 + n], start=True, stop=True),
                                        reads=[B_QIT[t], B_K], writes=[B_pI[h % 2]])
                                    ri = rbi[0] % 3
                                    rbi[0] += 1
                                    P.op("act", lambda e, pi=pi, ri=ri, n=n: e.activation(out=RB[ri][0:nq, 0:n], in_=pi[0:nq, 0:n], func=AF.Relu),
                                         reads=[B_pI[h % 2]], writes=[B_RB[ri]])
                                    if h == 0:
                                        P.op("dve", lambda e, ri=ri, c0=c0, n=n: e.tensor_scalar(
                                            out=SC[0:nq, c0:c0 + n], in0=RB[ri][0:nq, 0:n], scalar1=wi_ap[:, 0:1], scalar2=None, op0=ALU.mult),
                                            reads=[B_RB[ri], B_WI], writes=[B_SC])
                                    else:
                                        P.op("dve", lambda e, ri=ri, c0=c0, n=n, h=h: e.scalar_tensor_tensor(
                                            out=SC[0:nq, c0:c0 + n], in0=RB[ri][0:nq, 0:n], scalar=wi_ap[:, h:h + 1], in1=SC[0:nq, c0:c0 + n],
                                            op0=ALU.mult, op1=ALU.add), reads=[B_RB[ri], B_WI, B_SC], writes=[B_SC])
                            P.op("dve", lambda e: e.tensor_reduce(out=BS[0:nq, 0:1], in_=SC[0:nq, 0:S], axis=AX.X, op=ALU.max,
                                                                   apply_absolute_value=True), reads=[B_SC], writes=[B_BS])
                            if masked:
                                P.op("dve", lambda e: e.tensor_tensor(out=SC[0:nq, S - 256:S], in0=SC[0:nq, S - 256:S], in1=meo_s[0:nq, :],
                                                                        op=ALU.add), reads=[B_SC, B_const], writes=[B_SC])
                            P.op("dve", lambda e: e.tensor_scalar(out=BS[0:nq, 0:1], in0=BS[0:nq, 0:1], scalar1=1.0001, scalar2=1e-6,
                                                                   op0=ALU.mult, op1=ALU.add), reads=[B_BS], writes=[B_BS])
                            P.op("dve", lambda e: e.tensor_scalar(out=BS[0:nq, 9:10 + NBIS], in0=CK[0:nq, :], scalar1=BS[0:nq, 0:1],
                                                                   scalar2=None, op0=ALU.mult), reads=[B_BS, B_const], writes=[B_BS])
                            P.op("dve", lambda e: e.memset(BS[0:nq, 1:2], 0.0), reads=[B_BS], writes=[B_BS])
                        if part in ("all", "main"):
                            for k in range(1, NBIS + 1):
                                P.op("dve", lambda e: e.tensor_scalar(out=M01[0:nq, 0:S], in0=SC[0:nq, 0:S], scalar1=BS[0:nq, 1:2], scalar2=None,
                                                                       op0=ALU.is_ge, op1=ALU.add, accum_out=BS[0:nq, 2:3]),
                                     reads=[B_SC, B_BS], writes=[B_BS, B_M01])
                                nf = -(-len(bis_filler) // (NBIS + 1 - k)) if bis_filler else 0
                                for _ in range(-(-nf // 3)):
                                    if bis_filler:
                                        P.replay(bis_filler.pop(0))
                                P.op("dve", lambda e, k=k: e.scalar_tensor_tensor(out=BS[0:nq, 3:4], in0=BS[0:nq, 2:3], scalar=TOPK - 0.5,
                                                                                   in1=BS[0:nq, 8 + k:9 + k], op0=ALU.is_ge, op1=ALU.mult),
                                     reads=[B_BS], writes=[B_BS])
                                for _ in range(-(-nf // 3)):
                                    if bis_filler:
                                        P.replay(bis_filler.pop(0))
                                P.op("dve", lambda e, k=k: e.scalar_tensor_tensor(out=BS[0:nq, 1:2], in0=BS[0:nq, 3:4],
                                                                                   scalar=BS[0:nq, 9 + k:10 + k], in1=BS[0:nq, 1:2],
                                                                                   op0=ALU.subtract, op1=ALU.add),
                                     reads=[B_BS], writes=[B_BS])
                                for _ in range(max(0, nf - 2 * (-(-nf // 3)))):
                                    if bis_filler:
                                        P.replay(bis_filler.pop(0))
                            P.op("dve", lambda e: e.tensor_tensor(out=BS[0:nq, 1:2], in0=BS[0:nq, 1:2], in1=BS[0:nq, 9 + NBIS:10 + NBIS],
                                                                   op=ALU.subtract), reads=[B_BS], writes=[B_BS])
                            P.op("dve", lambda e: e.tensor_scalar(out=M01[0:nq, 0:S], in0=SC[0:nq, 0:S], scalar1=BS[0:nq, 1:2], scalar2=None,
                                                                   op0=ALU.is_ge), reads=[B_SC, B_BS], writes=[B_M01])
                            if not (A_MODE & 4):
                                return
                            LAG = 2
                            pend = {}
                            banks = [(pL[0], pL[1], B_pL[0], B_pL[1]), (pI[0], pI[1], B_pI[0], B_pI[1])]
                            W2 = 2 * nq

                            def stage_a(idx):
                                s = idx
                                mb = s // 8
                                mi = mb % 2
                                nk = last_keys if s == ntiles - 1 else 128
                                if s % 8 == 0:
                                    for s2_ in range(mb * 8, min(ntiles, mb * 8 + 8)):
                                        nk2 = last_keys if s2_ == ntiles - 1 else 128
                                        a_ = s2_ - mb * 8
                                        tgt = pM if a_ % 2 == 0 else pSb
                                        P.op("pe", lambda e, s2_=s2_, nk2=nk2, a_=a_, tgt=tgt: e.transpose(
                                            out=tgt[0:nk2, (a_ // 2) * 128:(a_ // 2) * 128 + nq], in_=M01[0:nq, s2_ * 128:s2_ * 128 + nk2],
                                            identity=ident_b[0:nq, 0:nq]), reads=[B_M01, B_const],
                                            writes=([B_pM] if a_ % 2 == 0 else [B_pS2[0], B_pS2[1]]))
                                    mtv2 = MT[mi][:].rearrange("p (a r) k -> p a r k", r=2)
                                    P.op("act", lambda e, mtv2=mtv2: e.activation(out=mtv2[:, :, 0, :], in_=pM[:, 0:512].rearrange("p (a k) -> p a k", k=128),
                                                                                  func=AF.Copy), reads=[B_pM], writes=[B_MT[mi]])
                                    if min(ntiles, mb * 8 + 8) - mb * 8 > 1:
                                        P.op("act", lambda e, mtv2=mtv2: e.activation(out=mtv2[:, :, 1, :], in_=pSb[:, 0:512].rearrange("p (a k) -> p a k", k=128),
                                                                                      func=AF.Copy), reads=[B_pS2[0], B_pS2[1]], writes=[B_MT[mi]])
                                bA, bB, B_A, B_B = banks[idx % 2]
                                for g in range(2):
                                    KG = K0 if g == 0 else K1
                                    for half in range(2):
                                        bank, B_bank = (bA, B_A) if half == 0 else (bB, B_B)
                                        if nq == 128:
                                            P.op("pe", lambda e, bank=bank, KG=KG, half=half, s=s, nk=nk, g=g: e.matmul(
                                                bank[0:nk, g * 256:(g + 1) * 256],
                                                lhsT=KG[half * 64:half * 64 + 64, s * 128:s * 128 + nk],
                                                rhs=QT[half * 64:half * 64 + 64, t, g * 256:(g + 1) * 256],
                                                start=True, stop=True), reads=[B_QT[t], B_K], writes=[B_bank])
                                        else:
                                            for cc in range(2):
                                                P.op("pe", lambda e, bank=bank, KG=KG, half=half, s=s, nk=nk, g=g, cc=cc: e.matmul(
                                                    bank[0:nk, g * W2 + cc * nq:g * W2 + (cc + 1) * nq],
                                                    lhsT=KG[half * 64:half * 64 + 64, s * 128:s * 128 + nk],
                                                    rhs=QT[half * 64:half * 64 + 64, t, (g * 2 + cc) * 128 + tok0:(g * 2 + cc) * 128 + tok0 + nq],
                                                    start=True, stop=True), reads=[B_QT[t], B_K], writes=[B_bank])
                                ei = idx % 3
                                ebv = EBA[ei][0:nk, 0:4 * W2].rearrange("p (g h k) -> p g h k", g=2, h=2)
                                for half, (bank, B_bank) in enumerate(((bA, B_A), (bB, B_B))):
                                    P.op("act", lambda e, bank=bank, half=half, ebv=ebv, nk=nk: e.activation(
                                        out=ebv[:, :, half, :], in_=bank[0:nk, 0:2 * W2].rearrange("p (g k) -> p g k", g=2),
                                        func=AF.Exp, scale=ATTN_SCALE), reads=[B_bank], writes=[B_EBA[ei]])
                                mtv = MT[mi][0:nk, s - mb * 8, 0:nq].unsqueeze(1).broadcast_to([nk, 8, nq])
                                P.op("dve", lambda e, ei=ei, nk=nk, mtv=mtv: e.tensor_tensor(
                                    out=PBA[ei][0:nk, 0:8 * nq].rearrange("p (a k) -> p a k", a=8),
                                    in0=EBA[ei][0:nk, 0:8 * nq].rearrange("p (a k) -> p a k", a=8), in1=mtv, op=ALU.mult),
                                    reads=[B_EBA[ei], B_MT[mi]], writes=[B_PBA[ei]])
                                pend[idx] = (ei, nk)

                            def stage_b(idx):
                                s = idx
                                ei, nk = pend.pop(idx)
                                for g in range(2):
                                    P.op("pe", lambda e, ei=ei, nk=nk, s=s, g=g: e.matmul(
                                        pO[g][0:65, 0:4 * nq], lhsT=VX[0:nk, s, g, :], rhs=PBA[ei][0:nk, g * 4 * nq:(g + 1) * 4 * nq],
                                        start=(s == 0), stop=(s == ntiles - 1)), reads=[B_PBA[ei], B_V], writes=[B_pO[g]])

                            nsteps = ntiles + LAG
                            for step in range(nsteps):
                                if step < ntiles:
                                    stage_a(step)
                                if step >= LAG:
                                    stage_b(step - LAG)
                                if filler:
                                    nf = -(-len(filler) // (nsteps - step))
                                    for _ in range(nf):
                                        filler.pop(0)()

                            for g in range(2):
                                P.op("act", lambda e, g=g: e.activation(out=OS[:, 0:4 * nq], in_=pO[g][0:65, 0:4 * nq], func=AF.Copy),
                                     reads=[B_pO[g]], writes=[B_OS])
                                P.op("dve", lambda e: e.reciprocal(out=OS[64:65, 0:4 * nq], in_=OS[64:65, 0:4 * nq]),
                                     reads=[B_OS], writes=[B_OS])
                                P.op("pe", lambda e: e.matmul(pS[0:64, 0:4 * nq], lhsT=sel_f[:], rhs=OS[:, 0:4 * nq], start=True, stop=True),
                                     reads=[B_OS, B_const], writes=[B_pS2[0], B_pS2[1]])
                                for a, hh in enumerate((4 * g, 4 * g + 2, 4 * g + 1, 4 * g + 3)):
                                    P.op("dve", lambda e, a=a, hh=hh: e.tensor_tensor(
                                        out=ATT[:, hh, att_tok0:att_tok0 + nq], in0=OS[0:64, a * nq:(a + 1) * nq], in1=pS[0:64, a * nq:(a + 1) * nq],
                                        op=ALU.mult), reads=[B_OS, B_pS2[0], B_pS2[1]], writes=[B_ATT])

                    def conv_ops(t):
                        lst = []
                        for k in range(CONVW):
                            for c in range(4):
                                for (off, n, t0) in useg(t):
                                    src = UT[:, c, off + 2 + k:off + 2 + k + n]
                                    if k == 0:
                                        lst.append(lambda c=c, src=src, t0=t0, n=n, k=k: P.op("dve", lambda e: e.tensor_scalar(
                                            out=CAC[:, c, t0:t0 + n], in0=src, scalar1=prm(l, O_CW + c * CONVW + k),
                                            scalar2=prm(l, O_CB + c), op0=ALU.mult, op1=ALU.add),
                                            reads=[B_UT, B_UTh, B_const], writes=[B_CACc[c], B_CAC]))
                                    else:
                                        lst.append(lambda c=c, src=src, t0=t0, n=n, k=k: P.op("dve", lambda e: e.scalar_tensor_tensor(
                                            out=CAC[:, c, t0:t0 + n], in0=src, scalar=prm(l, O_CW + c * CONVW + k),
                                            in1=CAC[:, c, t0:t0 + n], op0=ALU.mult, op1=ALU.add),
                                            reads=[B_UT, B_UTh, B_const, B_CACc[c]],
                                            writes=[B_CACc[c]] + ([B_CAC] if k == CONVW - 1 else [])))
                        return lst

                    def conv_tile(t):
                        for f_ in conv_ops(t):
                            f_()

                    def finish_tile(t):
                        def cons_conv(c, T_ap, B_T):
                            P.op("act", lambda e, c=c, T_ap=T_ap: e.activation(out=CVT[:, c, :], in_=T_ap, func=AF.Silu),
                                 reads=[B_T], writes=[B_CVT])
                        if F_MODE & 1:
                            ln_conv_run(cons_conv)
                        for dc in range(8):
                            if not (F_MODE & 2):
                                break
                            xi = dc % 2
                            P.dma("sp", lambda e, dc=dc, xi=xi: e.dma_start(out=XFc[xi][:], in_=XT[l][:, dc, t * 128:(t + 1) * 128]),
                                  reads=[B_XT[l][t]], writes=[B_XFc[xi]])
                            pw = pI[dc % 2]
                            for c in range(4):
                                P.op("pe", lambda e, pw=pw, c=c, dc=dc: e.matmul(pw[:, 0:128], lhsT=WoC[:, c, dc * 128:(dc + 1) * 128],
                                                                                 rhs=CVT[:, c, :], start=(c == 0), stop=False),
                                     reads=[B_Wo, B_CVT], writes=[B_pI[dc % 2]])
                            for h in range(8):
                                P.op("pe", lambda e, pw=pw, h=h, dc=dc: e.matmul(pw[:, 0:128], lhsT=WoA[:, h, dc * 128:(dc + 1) * 128],
                                                                                 rhs=ATT[:, h, :], start=False, stop=(h == 7)),
                                     reads=[B_Wo, B_ATT], writes=[B_pI[dc % 2]])
                            P.op("dve", lambda e, pw=pw, dc=dc, xi=xi: e.scalar_tensor_tensor(
                                out=Z[:, dc, :], in0=XFc[xi][:], scalar=ALPHA, in1=pw[:, 0:128], op0=ALU.mult, op1=ALU.add),
                                reads=[B_XFc[xi], B_pI[dc % 2]], writes=[B_Z])
                        xo_i = t % 2

                        def cons_ln1(c, T_ap, B_T):
                            P.op("pool", lambda e, c=c, T_ap=T_ap: e.tensor_copy(out=XNo[xo_i][:, c, :], in_=T_ap),
                                 reads=[B_T], writes=[B_XNo[xo_i]])
                        if F_MODE & 4:
                            ln_1_run(cons_ln1)
                        if F_MODE & 8:
                            P.dma("sp", lambda e: e.dma_start(out=XN[l][:, :, t * 128:(t + 1) * 128], in_=XNo[xo_i][:]),
                                  reads=[B_XNo[xo_i]], writes=[B_XN[l][t]])

                    def _mk(run_factory_args):
                        return None
                    def ln_conv_run(cons):
                        layernorm_cfg["lc"][0](cons)
                    def ln_1_run(cons):
                        layernorm_cfg["l1"][0](cons)
                    layernorm_cfg = {}

                    def make_ln(tag, Zt, B_Zt, nch, Dn, og, ob_):
                        SQ = [sbuf(st, "%s_sq%d" % (tag, i), [128, 128], F32) for i in range(2)]
                        B_SQ = [Buf() for _ in range(2)]
                        MEAN = sbuf(st, tag + "_mean", [128, 128], F32)
                        M2 = sbuf(st, tag + "_m2", [128, 128], F32)
                        RSTD = sbuf(st, tag + "_rstd", [128, 128], F32)
                        TT = [sbuf(st, "%s_t%d" % (tag, i), [128, 128], F32) for i in range(3)]
                        B_TT = [Buf() for _ in range(3)]
                        B_M, B_M2, B_R = Buf(), Buf(), Buf()
                        state = {"i": 0}
                        N = 128

                        def run(consume):
                            s1 = pS[:, 0:N]
                            s2 = pS[:, N:2 * N]
                            if LN_STEPS < 1:
                                return
                            for c in range(nch):
                                P.op("pe", lambda e, c=c: e.matmul(s1, lhsT=ones_f[:], rhs=Zt[:, c, :], start=(c == 0), stop=(c == nch - 1)),
                                     reads=[B_Zt, B_const], writes=[B_pS2[0]])
                            if LN_STEPS < 2:
                                return
                            for c in range(nch):
                                i = state["i"] % 2
                                state["i"] += 1
                                P.op("act", lambda e, c=c, i=i: e.activation(out=SQ[i][:], in_=Zt[:, c, :], func=AF.Square),
                                     reads=[B_Zt], writes=[B_SQ[i]])
                                P.op("pe", lambda e, c=c, i=i: e.matmul(s2, lhsT=ones_f[:], rhs=SQ[i][:], start=(c == 0), stop=(c == nch - 1)),
                                     reads=[B_SQ[i], B_const], writes=[B_pS2[1]])
                            if LN_STEPS < 3:
                                return
                            P.op("dve", lambda e: e.tensor_scalar(out=MEAN[:], in0=s1, scalar1=1.0 / Dn, scalar2=None, op0=ALU.mult),
                                 reads=[B_pS2[0], B_pS2[1]], writes=[B_M])
                            if LN_STEPS < 4:
                                return
                            P.op("dve", lambda e: e.tensor_tensor(out=M2[:], in0=MEAN[:], in1=MEAN[:], op=ALU.mult),
                                 reads=[B_M], writes=[B_M2])
                            P.op("dve", lambda e: e.scalar_tensor_tensor(out=M2[:], in0=s2, scalar=1.0 / Dn, in1=M2[:],
                                                                         op0=ALU.mult, op1=ALU.subtract),
                                 reads=[B_pS2[1], B_M2], writes=[B_M2])
                            P.op("dve", lambda e: e.tensor_scalar(out=M2[:], in0=M2[:], scalar1=0.0, scalar2=LN_EPS,
                                                                  op0=ALU.max, op1=ALU.add), reads=[B_M2], writes=[B_M2])
                            if LN_STEPS < 6:
                                return
                            P.op("act", lambda e: e.activation(out=RSTD[:], in_=M2[:], func=AF.Sqrt), reads=[B_M2], writes=[B_R])
                            P.op("dve", lambda e: e.reciprocal(out=RSTD[:], in_=RSTD[:]), reads=[B_R], writes=[B_R])
                            if LN_STEPS < 8:
                                return
                            for c in range(nch):
                                i = state["i"] % 3
                                state["i"] += 1
                                P.op("dve", lambda e, c=c, i=i: e.tensor_tensor(out=TT[i][:], in0=Zt[:, c, :], in1=MEAN[:], op=ALU.subtract),
                                     reads=[B_Zt, B_M], writes=[B_TT[i]])
                                P.op("dve", lambda e, c=c, i=i: e.tensor_tensor(out=TT[i][:], in0=TT[i][:], in1=RSTD[:], op=ALU.mult),
                                     reads=[B_R, B_TT[i]], writes=[B_TT[i]])
                                P.op("dve", lambda e, c=c, i=i: e.tensor_scalar(out=TT[i][:], in0=TT[i][:], scalar1=prm(l, og + c),
                                                                                scalar2=prm(l, ob_ + c), op0=ALU.mult, op1=ALU.add),
                                     reads=[B_TT[i], B_const], writes=[B_TT[i]])
                                consume(c, TT[i][:], B_TT[i])
                        layernorm_cfg[tag] = (run,)

                    make_ln("lc", CAC, B_CAC, 4, 512.0, O_CG, O_CBB)
                    make_ln("l1", Z, B_Z, 8, 1024.0, O_L1G, O_L1B)
                    print("phase A sbuf remaining", nc.sbuf_bytes_remaining)

                    for s_ in range(2):
                        P.dma("sp", lambda e, s_=s_: e.dma_start(out=SCV[0:30, :], in_=sconv_d[l, s_, :, :]), writes=[B_SCV])
                        for c in range(4):
                            P.op("pe", lambda e, c=c: e.transpose(out=pS[:, c * 32:c * 32 + 30], in_=SCV[0:30, c * 128:(c + 1) * 128],
                                                                  identity=ident_f[0:30, 0:30]),
                                 reads=[B_SCV, B_const], writes=[B_pS2[0], B_pS2[1]])
                        off = NPT * 160 + s_ * 96
                        P.op("act", lambda e, off=off: e.activation(out=UT[:, :, off + 2:off + 32],
                                                                    in_=pS[:, 0:128].rearrange("p (c k) -> p c k", c=4)[:, :, 0:30], func=AF.Copy),
                             reads=[B_pS2[0], B_pS2[1]], writes=[B_UTh])

                    ld = [0]

                    def load_sample_keys(s_):
                        for q4 in range(4):
                            i = ld[0] % 2
                            ld[0] += 1
                            r0 = q4 * 1024
                            P.dma("pool", lambda e, i=i, r0=r0: e.dma_start(
                                out=CKS[i][:], in_=ck_d[l, s_, r0:r0 + 1024, :].rearrange("(a p) k -> p a k", p=128)), writes=[B_CKS[i]])
                            P.dma("pool", lambda e, i=i, r0=r0: e.dma_start(
                                out=CIS[i][:], in_=cki_d[l, s_, r0:r0 + 1024, :].rearrange("(a p) k -> p a k", p=128)), writes=[B_CIS[i]])
                            P.dma("pool", lambda e, i=i, r0=r0: e.dma_start(
                                out=VST[i][:], in_=cv_d[l, s_, r0:r0 + 1024, :].rearrange("(a p) k -> p a k", p=128)), writes=[B_VST[i]])
                            kd_o = CKD[i][:, :, 0:256].rearrange("p a (g r d) -> p a g r d", g=2, r=2)
                            kd_i = CKS[i][:].rearrange("p a (g d) -> p a g d", g=2).unsqueeze(3).broadcast_to([128, 8, 2, 2, 64])
                            P.op("dve", lambda e, kd_o=kd_o, kd_i=kd_i: e.tensor_copy(out=kd_o, in_=kd_i),
                                 reads=[B_CKS[i]], writes=[B_CKD[i]])
                            ki_o = CKD[i][:, :, 256:384].rearrange("p a (r d) -> p a r d", r=2)
                            ki_i = CIS[i][:].unsqueeze(2).broadcast_to([128, 8, 2, 64])
                            P.op("dve", lambda e, ki_o=ki_o, ki_i=ki_i: e.tensor_copy(out=ki_o, in_=ki_i),
                                 reads=[B_CIS[i]], writes=[B_CKD[i]])
                            P.op("dve", lambda e, i=i, q4=q4: e.tensor_copy(out=VX[:, q4 * 8:(q4 + 1) * 8, :, 0:64],
                                                                            in_=VST[i][:].rearrange("p a (g d) -> p a g d", g=2)),
                                 reads=[B_VST[i]], writes=[B_V])
                            for a3, KD in enumerate((K0, K1, KI)):
                                for a in range(8):
                                    tgt = pM if a % 2 == 0 else pSb
                                    P.op("pe", lambda e, a=a, a3=a3, i=i, tgt=tgt: e.transpose(out=tgt[:, (a // 2) * 128:(a // 2 + 1) * 128],
                                                                                              in_=CKD[i][:, a, a3 * 128:(a3 + 1) * 128], identity=ident_b[:]),
                                         reads=[B_CKD[i], B_const], writes=([B_pM] if a % 2 == 0 else [B_pS2[0], B_pS2[1]]))
                                kdv = KD[:, q4 * 1024:(q4 + 1) * 1024].rearrange("p (a r k) -> p a r k", r=2, k=128)
                                P.op("act", lambda e, kdv=kdv: e.activation(out=kdv[:, :, 0, :], in_=pM[:, 0:512].rearrange("p (a k) -> p a k", k=128),
                                                                            func=AF.Copy), reads=[B_pM], writes=[B_K])
                                P.op("act", lambda e, kdv=kdv: e.activation(out=kdv[:, :, 1, :], in_=pSb[:, 0:512].rearrange("p (a k) -> p a k", k=128),
                                                                            func=AF.Copy), reads=[B_pS2[0], B_pS2[1]], writes=[B_K])
                        for a3, KD in enumerate((K0, K1, KI)):
                            P.op("dve", lambda e, KD=KD, a3=a3: e.tensor_copy(out=KD[:, 4096:4160], in_=SNK[:, a3, s_ * 64:s_ * 64 + 64]),
                                 reads=[B_SN], writes=[B_K])
                        P.op("dve", lambda e: e.tensor_copy(out=VX[0:64, 32, :, 0:64], in_=SNV[:, s_, :].rearrange("p (g d) -> p g d", g=2)),
                             reads=[B_SN], writes=[B_V])

                    fl16 = conv_ops(NT - 1) if (A_MODE & 16) else []
                    for s_ in range(2):
                        if not (A_MODE & 1):
                            break
                        load_sample_keys(s_)
                        wi_ap = WI[0:64, NT - 1, :] if s_ == 0 else WI[0:64, NT, :]
                        if A_MODE & 8:
                            attend(NT - 1, s_ * 64, 64, wi_ap, 33, 64, False, s_ * 64,
                                   filler=(fl16 if s_ == 1 else None))
                    for f_ in fl16:
                        f_()
                    fl16 = []
                    pending = []
                    if A_MODE & 32:
                        P.defer = []
                        finish_tile(NT - 1)
                        pending = P.defer
                        P.defer = None

                    for r in range(2):
                        for a3, KD in enumerate((K0, K1, KI)):
                            P.dma("sp", lambda e, r=r, a3=a3, KD=KD: e.dma_start(
                                out=KD[:, 0:4096].rearrange("p (j r k) -> p j r k", r=2, k=128)[:, :, r, :],
                                in_=ob_ap(l, r, a3 * 2048, 2048).rearrange("p (j k) -> p j k", k=128)),
                                reads=[B_OB[l]], writes=[B_K])
                        P.dma("sp", lambda e, r=r: e.dma_start(out=RTL[:, r, :], in_=ob_ap(l, r, 8192, 2048)),
                              reads=[B_OB[l]], writes=[B_RTL])
                        for q2 in range(2):
                            i = ld[0] % 2
                            ld[0] += 1
                            P.dma("sp", lambda e, r=r, q2=q2, i=i: e.dma_start(
                                out=VST[i][:], in_=ob_ap(l, r, 6144 + q2 * 1024, 1024).rearrange("p (a k) -> p a k", k=128)),
                                reads=[B_OB[l]], writes=[B_VST[i]])
                            P.op("dve", lambda e, r=r, q2=q2, i=i: e.tensor_copy(
                                out=VX[:, 0:32, :, 0:64].rearrange("p (j r) g d -> p j r g d", r=2)[:, q2 * 8:(q2 + 1) * 8, r, :, :],
                                in_=VST[i][:].rearrange("p a (g d) -> p a g d", g=2)), reads=[B_VST[i]], writes=[B_V])
                    r0v = RTL[:, 0, :].rearrange("p (c j k) -> p c j k", c=4, k=32)
                    r1v = RTL[:, 1, :].rearrange("p (c j k) -> p c j k", c=4, k=32)
                    uth = UT[:, :, 0:NPT * 160].rearrange("p c (j k) -> p c j k", k=160)
                    for c in range(4):
                        P.op("dve", lambda e, c=c: e.tensor_scalar(out=uth[:, c, :, 0:32], in0=r0v[:, c, :, :], scalar1=flags_s[:, 1:2],
                                                                   scalar2=None, op0=ALU.mult),
                             reads=[B_RTL, B_const], writes=[B_UTh])
                        P.op("dve", lambda e, c=c: e.scalar_tensor_tensor(out=uth[:, c, 1:NPT, 0:32], in0=r1v[:, c, 0:NPT - 1, :],
                                                                          scalar=flags_s[:, 0:1], in1=uth[:, c, 1:NPT, 0:32],
                                                                          op0=ALU.mult, op1=ALU.add),
                             reads=[B_RTL, B_const, B_UTh], writes=[B_UTh])
                    for j in range(NPT):
                        if not (A_MODE & 2):
                            break
                        fl = conv_ops(j) if (A_MODE & 16) else []
                        if A_MODE & 8:
                            if j == 0:
                                attend(j, 0, 128, WI[:, j, :], 2 * j + 2, 128, True, 0, part="pre")
                            attend(j, 0, 128, WI[:, j, :], 2 * j + 2, 128, True, 0, filler=fl, bis_filler=pending, part="main")
                            if j + 1 < NPT:
                                attend(j + 1, 0, 128, WI[:, j + 1, :], 2 * j + 4, 128, True, 0, part="pre")
                            attend(j, 0, 128, WI[:, j, :], 2 * j + 2, 128, True, 0, part="post")
                        for it_ in pending:
                            P.replay(it_)
                        pending = []
                        for f_ in fl:
                            f_()
                        if A_MODE & 32:
                            P.defer = []
                            finish_tile(j)
                            pending = P.defer
                            P.defer = None
                    for it_ in pending:
                        P.replay(it_)
                    pending = []
                    stop = end_phase()
                if stop:
                    break

            with contextlib.ExitStack() as st:
                NG = 256
                Wg = sbuf(st, "Wg", [128, 8, 2 * DFF], BF16)
                Wd = sbuf(st, "Wd", [128, NFC, D], BF16)
                B_Wg = [Buf() for _ in range(8)]
                B_Wd = [Buf() for _ in range(NFC)]
                XB = [sbuf(st, "XB%d" % i, [128, 8, NG], BF16) for i in range(2)]
                XF = [sbuf(st, "XF%d" % i, [128, 8, NG], F32) for i in range(2)]
                B_XB = [Buf() for _ in range(2)]
                B_XF = [Buf() for _ in range(2)]
                AT = sbuf(st, "AT", [128, NFC, NG], BF16)
                B_AT = [Buf() for _ in range(NFC)]
                GS = [sbuf(st, "GS%d" % i, [128, NG], F32) for i in range(2)]
                B_GS = [Buf() for _ in range(2)]
                Z2 = sbuf(st, "Z2", [128, 8, NG], F32)
                B_Z2 = Buf()
                XO = [sbuf(st, "XO0", [128, 8, NG], F32)] * 2
                B_XO = [Buf()] * 2
                YO = [sbuf(st, "YO0", [128, D], F32)] * 2
                B_YO = [Buf()] * 2
                pG = [psum(st, "pG%d" % i, [128, 512], F32) for i in range(2)]
                pGu = [psum(st, "pGu%d" % i, [128, 512], F32) for i in range(2)]
                B_pGu = [Buf() for _ in range(2)]
                pD = [psum(st, "pD%d" % i, [128, 512], F32) for i in range(2)]
                pS = psum(st, "pSF", [128, 512], F32)
                pY = [psum(st, "pY%d" % i, [128, 4, 128], F32) for i in range(1)]
                B_pG = [Buf() for _ in range(2)]
                B_pD = [Buf() for _ in range(4)]
                B_pS2 = [Buf(), Buf()]
                B_pY = [Buf()]
                SQ = [sbuf(st, "f_sq%d" % i, [128, NG], F32) for i in range(2)]
                B_SQ = [Buf() for _ in range(2)]
                MEAN = sbuf(st, "f_mean", [128, NG], F32)
                M2 = sbuf(st, "f_m2", [128, NG], F32)
                RSTD = sbuf(st, "f_rstd", [128, NG], F32)
                TT = [sbuf(st, "f_t%d" % i, [128, NG], F32) for i in range(2)]
                B_TT = [Buf() for _ in range(2)]
                B_M, B_M2, B_R = Buf(), Buf(), Buf()

                print("phase F sbuf remaining", nc.sbuf_bytes_remaining)
                wg_src = w_gu_d.ap()[l].rearrange("(c p) n -> p c n", p=128)
                wd_src = w_dn_d.ap()[l].rearrange("(f p) n -> p f n", p=128)
                NWB = (NFC + 3) // 4
                B_Wgb = {(kd, b): Buf() for kd in "gu" for b in range(NWB)}

                def issue_weights():
                    for b in range(NWB):
                        c0, c1 = b * 512, min((b + 1) * 512, DFF)
                        for kd, base in (("g", 0), ("u", DFF)):
                            P.dma("pool", lambda e, c0=c0, c1=c1, base=base: e.dma_start(out=Wg[:, :, base + c0:base + c1],
                                                                                       in_=wg_src[:, :, base + c0:base + c1]),
                                  writes=[B_Wgb[(kd, b)]])
                    for f in range(NFC):
                        P.dma("pool", lambda e, f=f: e.dma_start(out=Wd[:, f, :], in_=wd_src[:, f, :]), writes=[B_Wd[f]])

                groups = [(g * NG, NG) for g in range(NPT * 128 // NG)] + [(NPT * 128, 128)]
                sqi = [0]
                def issue_loads(gi_):
                    t0_, n_ = groups[gi_]
                    i_ = gi_ % 2
                    rd_ = [B_XN[l][t_] for t_ in range(t0_ // 128, (t0_ + n_) // 128)]
                    P.dma("pool", lambda e: e.dma_start(out=XB[i_][:, :, 0:n_], in_=XN[l][:, :, t0_:t0_ + n_]),
                          reads=rd_, writes=[B_XB[i_]])
                    P.dma("sp", lambda e: e.dma_start(out=XF[i_][:, :, 0:n_], in_=XN[l][:, :, t0_:t0_ + n_]),
                          reads=rd_, writes=[B_XF[i_]])

                issue_loads(0)
                issue_weights()
                pendF = []
                for gi, (t0, n) in enumerate(groups):
                    i = gi % 2
                    tl = list(range(t0 // 128, (t0 + n) // 128))
                    for f in range(NFC):
                        if pendF:
                            nf_ = -(-len(pendF) // (NFC - f))
                            for _ in range(nf_):
                                P.replay(pendF.pop(0))
                        pg = pG[f % 2]
                        for c in range(8):
                            P.op("pe", lambda e, pg=pg, f=f, c=c, i=i, n=n: e.matmul(
                                pg[:, 0:n], lhsT=Wg[:, c, f * 128:(f + 1) * 128], rhs=XB[i][:, c, 0:n], start=(c == 0), stop=(c == 7)),
                                reads=[B_XB[i], B_Wgb[("g", f // 4)]], writes=[B_pG[f % 2]])
                        for c in range(8):
                            P.op("pe", lambda e, pg=pg, f=f, c=c, i=i, n=n: e.matmul(
                                pGu[f % 2][:, 0:n], lhsT=Wg[:, c, DFF + f * 128:DFF + (f + 1) * 128], rhs=XB[i][:, c, 0:n],
                                start=(c == 0), stop=(c == 7)), reads=[B_XB[i], B_Wgb[("u", f // 4)]], writes=[B_pGu[f % 2]])
                        P.op("act", lambda e, pg=pg, f=f, n=n: e.activation(out=GS[f % 2][:, 0:n], in_=pg[:, 0:n], func=AF.Silu),
                             reads=[B_pG[f % 2]], writes=[B_GS[f % 2]])
                        P.op("dve", lambda e, f=f, n=n: e.tensor_tensor(out=AT[:, f, 0:n], in0=pGu[f % 2][:, 0:n], in1=GS[f % 2][:, 0:n],
                                                                        op=ALU.mult), reads=[B_pGu[f % 2], B_GS[f % 2]], writes=[B_AT[f]])
                    if gi + 1 < len(groups):
                        issue_loads(gi + 1)
                    for dc in range(8):
                        pd = pD[dc % 2]
                        for f in range(NFC):
                            P.op("pe", lambda e, pd=pd, f=f, dc=dc, n=n: e.matmul(
                                pd[:, 0:n], lhsT=Wd[:, f, dc * 128:(dc + 1) * 128], rhs=AT[:, f, 0:n], start=(f == 0), stop=(f == NFC - 1)),
                                reads=[B_AT[f], B_Wd[f]], writes=[B_pD[dc % 2]])
                        P.op("dve", lambda e, pd=pd, dc=dc, i=i, n=n: e.scalar_tensor_tensor(
                            out=Z2[:, dc, 0:n], in0=XF[i][:, dc, 0:n], scalar=ALPHA, in1=pd[:, 0:n], op0=ALU.mult, op1=ALU.add),
                            reads=[B_XF[i], B_pD[dc % 2]], writes=[B_Z2])
                    P.defer = []
                    s1 = pS[:, 0:n]
                    s2 = pS[:, 256:256 + n]
                    for c in range(8):
                        P.op("pe", lambda e, c=c, n=n, s1=s1: e.matmul(s1, lhsT=ones_f[:], rhs=Z2[:, c, 0:n], start=(c == 0), stop=(c == 7)),
                             reads=[B_Z2, B_const], writes=[B_pS2[0]])
                    for c in range(8):
                        q = sqi[0] % 2
                        sqi[0] += 1
                        P.op("act", lambda e, c=c, q=q, n=n: e.activation(out=SQ[q][:, 0:n], in_=Z2[:, c, 0:n], func=AF.Square),
                             reads=[B_Z2], writes=[B_SQ[q]])
                        P.op("pe", lambda e, c=c, q=q, n=n, s2=s2: e.matmul(s2, lhsT=ones_f[:], rhs=SQ[q][:, 0:n], start=(c == 0), stop=(c == 7)),
                             reads=[B_SQ[q], B_const], writes=[B_pS2[1]])
                    P.op("dve", lambda e, n=n, s1=s1: e.tensor_scalar(out=MEAN[:, 0:n], in0=s1, scalar1=1.0 / D, scalar2=None, op0=ALU.mult),
                         reads=[B_pS2[0], B_pS2[1]], writes=[B_M])
                    P.op("dve", lambda e, n=n: e.tensor_tensor(out=M2[:, 0:n], in0=MEAN[:, 0:n], in1=MEAN[:, 0:n], op=ALU.mult),
                         reads=[B_M], writes=[B_M2])
                    P.op("dve", lambda e, n=n, s2=s2: e.scalar_tensor_tensor(out=M2[:, 0:n], in0=s2, scalar=1.0 / D, in1=M2[:, 0:n],
                                                                             op0=ALU.mult, op1=ALU.subtract),
                         reads=[B_pS2[1], B_M2], writes=[B_M2])
                    P.op("dve", lambda e, n=n: e.tensor_scalar(out=M2[:, 0:n], in0=M2[:, 0:n], scalar1=0.0, scalar2=LN_EPS,
                                                               op0=ALU.max, op1=ALU.add), reads=[B_M2], writes=[B_M2])
                    P.op("act", lambda e, n=n: e.activation(out=RSTD[:, 0:n], in_=M2[:, 0:n], func=AF.Sqrt), reads=[B_M2], writes=[B_R])
                    P.op("dve", lambda e, n=n: e.reciprocal(out=RSTD[:, 0:n], in_=RSTD[:, 0:n]), reads=[B_R], writes=[B_R])
                    for c in range(8):
                        q = c % 2
                        P.op("dve", lambda e, c=c, q=q, n=n: e.tensor_tensor(out=TT[q][:, 0:n], in0=Z2[:, c, 0:n], in1=MEAN[:, 0:n], op=ALU.subtract),
                             reads=[B_Z2, B_M], writes=[B_TT[q]])
                        P.op("dve", lambda e, c=c, q=q, n=n: e.tensor_tensor(out=TT[q][:, 0:n], in0=TT[q][:, 0:n], in1=RSTD[:, 0:n], op=ALU.mult),
                             reads=[B_R, B_TT[q]], writes=[B_TT[q]])
                        P.op("dve", lambda e, c=c, q=q, n=n, i=i: e.tensor_scalar(out=XO[i][:, c, 0:n], in0=TT[q][:, 0:n], scalar1=prm(l, O_L2G + c),
                                                                                   scalar2=prm(l, O_L2B + c), op0=ALU.mult, op1=ALU.add),
                             reads=[B_TT[q], B_const], writes=[B_XO[i]])
                    if l < DEPTH - 1:
                        for t in tl:
                            P.dma("sp", lambda e, i=i, t=t, t0=t0: e.dma_start(out=XT[l + 1][:, :, t * 128:(t + 1) * 128],
                                                                               in_=XO[i][:, :, t * 128 - t0:(t + 1) * 128 - t0]),
                                  reads=[B_XO[i]], writes=[B_XT[l + 1][t]])
                    else:
                        for t in tl:
                            yi = t % 2
                            for hf in range(2):
                                for c in range(4):
                                    P.op("pe", lambda e, c=c, hf=hf, i=i, t=t, t0=t0: e.transpose(
                                        out=pY[0][:, c, :], in_=XO[i][:, hf * 4 + c, t * 128 - t0:(t + 1) * 128 - t0], identity=ident_f[:]),
                                        reads=[B_XO[i], B_const], writes=[B_pY[0]])
                                P.op("act", lambda e, yi=yi, hf=hf: e.activation(out=YO[yi][:, hf * 512:(hf + 1) * 512],
                                                                                 in_=pY[0][:].rearrange("p c k -> p (c k)"), func=AF.Copy),
                                     reads=[B_pY[0]], writes=[B_YO[yi]])
                            P.dma("sp", lambda e, yi=yi, t=t: e.dma_start(out=y_d[t * 128:(t + 1) * 128, :], in_=YO[yi][:]),
                                  reads=[B_YO[yi]], writes=[Buf()])
                    pendF = P.defer
                    P.defer = None
                for it_ in pendF:
                    P.replay(it_)
                pendF = []
                stop = end_phase()
        print("program instructions:", P.n_instr)
    return nc


_NC_CACHE = {}


def _rope_tables():
    half = 32
    inv = (10000.0 ** (-np.arange(half, dtype=np.float32) / half)).astype(np.float32)
    return inv


def kernel(x_prompt, x_sample, cache_k, cache_v, cache_k_idx, state_conv,
           w_in, conv_w, conv_b, conv_ln_g, conv_ln_b, w_o, ln1_g, ln1_b,
           w_gate_up, w_down, ln2_g, ln2_b):
    f32 = np.float32
    x_prompt = np.asarray(x_prompt, f32)
    x_sample = np.asarray(x_sample, f32)
    cache_k = np.asarray(cache_k, f32)
    cache_v = np.asarray(cache_v, f32)
    cache_k_idx = np.asarray(cache_k_idx, f32)
    state_conv = np.asarray(state_conv, f32)
    if "nc" not in _NC_CACHE:
        _NC_CACHE["nc"] = build_program()
    nc = _NC_CACHE["nc"]

    inv = _rope_tables()
    prm = np.zeros((DEPTH, 128, NPRM), f32)
    for l in range(DEPTH):
        cw = np.asarray(conv_w[l], f32).reshape(CONVW, 4, 128)
        prm[l, :, 0:124] = cw.transpose(2, 1, 0).reshape(128, 124)
        prm[l, :, 124:128] = np.asarray(conv_b[l], f32).reshape(4, 128).T
        prm[l, :, 128:132] = np.asarray(conv_ln_g[l], f32).reshape(4, 128).T
        prm[l, :, 132:136] = np.asarray(conv_ln_b[l], f32).reshape(4, 128).T
        prm[l, :, 136:144] = np.asarray(ln1_g[l], f32).reshape(8, 128).T
        prm[l, :, 144:152] = np.asarray(ln1_b[l], f32).reshape(8, 128).T
        prm[l, :, 152:160] = np.asarray(ln2_g[l], f32).reshape(8, 128).T
        prm[l, :, 160:168] = np.asarray(ln2_b[l], f32).reshape(8, 128).T
    w_in = np.ascontiguousarray(np.asarray(w_in, f32))
    w_o = np.ascontiguousarray(np.asarray(w_o, f32))
    w_gu = np.ascontiguousarray(np.asarray(w_gate_up, f32))
    w_dn = np.ascontiguousarray(np.asarray(w_down, f32))

    in_maps = []
    for c in range(8):
        b, r = c // 2, c % 2
        xo = np.empty((TOK, D), f32)
        pos = np.empty((NT, 128), np.int64)
        for j in range(NPT):
            g = 2 * j + r
            xo[j * 128:(j + 1) * 128] = x_prompt[b, g * 128:(g + 1) * 128]
            pos[j] = g * 128 + np.arange(128)
        xo[NPT * 128:NPT * 128 + 64] = x_sample[2 * c]
        xo[NPT * 128 + 64:] = x_sample[2 * c + 1]
        pos[NPT, :64] = SEQ + np.arange(64)
        pos[NPT, 64:] = SEQ + np.arange(64)
        ang = pos.astype(f32)[:, :, None] * inv[None, None, :]
        rope = np.empty((128, NT, 96), f32)
        rope[:, :, 0:32] = np.cos(ang).transpose(1, 0, 2)
        rope[:, :, 32:64] = np.sin(ang).transpose(1, 0, 2)
        rope[:, :, 64:96] = -np.sin(ang).transpose(1, 0, 2)
        qi = np.arange(128)[:, None] // 64
        si = np.arange(128)[None, :] // 64
        causal = np.where(si <= qi, 0.0, NEG).astype(f32)
        meo = np.empty((128, 256), f32)
        if r == 0:
            meo[:, 0:128] = causal
            meo[:, 128:256] = NEG
        else:
            meo[:, 0:128] = 0.0
            meo[:, 128:256] = causal
        flags = np.zeros((128, 2), f32)
        flags[:, r] = 1.0
        in_maps.append({
            "x_own": xo, "rope": rope, "meo": meo, "flags": flags,
            "ck": np.ascontiguousarray(cache_k[:, 2 * c:2 * c + 2].reshape(DEPTH, 2, SEQ, 128)),
            "cv": np.ascontiguousarray(cache_v[:, 2 * c:2 * c + 2].reshape(DEPTH, 2, SEQ, 128)),
            "cki": np.ascontiguousarray(cache_k_idx[:, 2 * c:2 * c + 2]),
            "sconv": np.ascontiguousarray(state_conv[:, 2 * c:2 * c + 2]),
            "w_in": w_in, "prm": prm, "w_o": w_o, "w_gu": w_gu, "w_dn": w_dn,
        })

    res = run_bass_kernel_spmd(nc, in_maps, core_ids=list(range(8)))
    outs = res.results

    y_p = np.empty((4, SEQ, D), f32)
    y_s = np.empty((16, 64, D), f32)
    nk_p = np.empty((DEPTH, 4, SEQ, 2, 64), f32)
    nv_p = np.empty((DEPTH, 4, SEQ, 2, 64), f32)
    nki_p = np.empty((DEPTH, 4, SEQ, 64), f32)
    ncv_p = np.empty((DEPTH, 4, 30, CC), f32)
    nk_s = np.empty((DEPTH, 16, 64, 2, 64), f32)
    nv_s = np.empty((DEPTH, 16, 64, 2, 64), f32)
    nki_s = np.empty((DEPTH, 16, 64, 64), f32)
    ncv_s = np.empty((DEPTH, 16, 30, CC), f32)
    for c in range(8):
        b, r = c // 2, c % 2
        o = outs[c]
        y, nk, nv, nki, ncv = o["y"], o["nk"], o["nv"], o["nki"], o["ncv"]
        for j in range(NPT):
            g = 2 * j + r
            sl = slice(g * 128, (g + 1) * 128)
            y_p[b, sl] = y[j * 128:(j + 1) * 128]
            nk_p[:, b, sl] = nk[:, j * 128:(j + 1) * 128].reshape(DEPTH, 128, 2, 64)
            nv_p[:, b, sl] = nv[:, j * 128:(j + 1) * 128].reshape(DEPTH, 128, 2, 64)
            nki_p[:, b, sl] = nki[:, j * 128:(j + 1) * 128]
        if r == 1:
            ncv_p[:, b] = ncv[:, 0]
        for s_ in range(2):
            q = 2 * c + s_
            rows = slice(NPT * 128 + s_ * 64, NPT * 128 + (s_ + 1) * 64)
            y_s[q] = y[rows]
            nk_s[:, q] = nk[:, rows].reshape(DEPTH, 64, 2, 64)
            nv_s[:, q] = nv[:, rows].reshape(DEPTH, 64, 2, 64)
            nki_s[:, q] = nki[:, rows]
            ncv_s[:, q] = ncv[:, 1 + s_]
    return (y_p, y_s, nk_p, nv_p, nki_p, ncv_p, nk_s, nv_s, nki_s, ncv_s)
```
